# Optimizing a Trainium2 kernel written in Bass

```python
import math
import jax, jax.numpy as jnp
from jax import lax
import numpy as np

D_MODEL = 1024
BATCH = 2
SEQ = 16384
DEPTH = 4

N_EVEN = (DEPTH + 1) // 2
N_ODD = DEPTH // 2
D_FF = 1408
N_FFN = 2
EPS = 1e-6
Q_BLOCK = 128

DIFF_HEADS = 4
DIFF_DH = 64
DIFF_DV = 2 * DIFF_DH
GLA_HEADS = 4
GLA_DK = 64
GLA_DV = 128
GLA_RANK = 16
GLA_TAU = 16.0
GLA_CHUNK = 64
SB_HEADS = 8
SB_DH = D_MODEL // SB_HEADS

HYB_SPLITS = (DIFF_HEADS * 2 * DIFF_DH, DIFF_HEADS * 2 * DIFF_DH, DIFF_HEADS * DIFF_DV,
              GLA_HEADS * GLA_DK, GLA_HEADS * GLA_DK, GLA_HEADS * GLA_DV, GLA_HEADS * GLA_DV, GLA_RANK)
HYB_IN = sum(HYB_SPLITS)
HYB_OUT = DIFF_HEADS * DIFF_DV + GLA_HEADS * GLA_DV

kernel_name = "hybrid_diffattn_gla_stickbreak_macaron"


def rmsnorm(x, g):
    xf = x.astype(jnp.float32)
    y = xf * lax.rsqrt(jnp.mean(xf * xf, axis=-1, keepdims=True) + EPS)
    return (y * g.astype(jnp.float32)).astype(x.dtype)


def swiglu(x, wg, wu, wd):
    return (jax.nn.silu(x @ wg) * (x @ wu)) @ wd


def to_blocks(t, size):
    b, s = t.shape[:2]
    t = t.reshape((b, s // size, size) + t.shape[2:])
    return jnp.moveaxis(t, 1, 0)


def from_blocks(t):
    t = jnp.moveaxis(t, 0, 1)
    return t.reshape((t.shape[0], t.shape[1] * t.shape[2]) + t.shape[3:])


def strict_upper(n, dtype):
    r = jnp.arange(n)
    return (r[:, None] > r[None, :]).astype(dtype)


def diff_attention(q, k, v, lam_params, subln_g, lambda_init):
    b, s = q.shape[:2]
    f32 = jnp.float32
    lp = lam_params.astype(f32)
    lam = jnp.exp(jnp.sum(lp[0] * lp[1])) - jnp.exp(jnp.sum(lp[2] * lp[3])) + lambda_init
    qf = q.astype(f32) * (DIFF_DH ** -0.5)
    kf = k.astype(f32)
    vf = v.astype(f32)
    outs = []
    for i in range(s // Q_BLOCK):
        start, end = i * Q_BLOCK, (i + 1) * Q_BLOCK
        sc = jnp.einsum('bqhd,bkhd->bhqk', qf[:, start:end], kf[:, :end])
        mask = jnp.arange(end)[None, :] <= (start + jnp.arange(Q_BLOCK))[:, None]
        p = jax.nn.softmax(jnp.where(mask, sc, -jnp.inf), axis=-1)
        p = p.reshape(b, DIFF_HEADS, 2, Q_BLOCK, end)
        p = p[:, :, 0] - lam * p[:, :, 1]
        outs.append(jnp.einsum('bhqk,bkhv->bqhv', p, vf[:, :end]))
    o = jnp.concatenate(outs, axis=1)
    o = rmsnorm(o, subln_g) * (1.0 - lambda_init)
    return o.reshape(b, s, DIFF_HEADS * DIFF_DV).astype(q.dtype)


def gla_chunked(q, k, v, log_a):
    b = q.shape[0]
    tri = jnp.tril(jnp.ones((GLA_CHUNK, GLA_CHUNK), dtype=bool))

    def prep(t):
        return jnp.moveaxis(to_blocks(t, GLA_CHUNK), 3, 2)

    def step(state, inp):
        qc, kc, vc, lac = inp
        cum = jnp.cumsum(lac, axis=2)
        o_inter = jnp.einsum('bhtk,bhkv->bhtv', qc * jnp.exp(cum), state)
        rel = cum[:, :, :, None, :] - cum[:, :, None, :, :]
        decay = jnp.exp(jnp.where(tri[:, :, None], rel, -jnp.inf))
        attn = jnp.einsum('bhtk,bhsk,bhtsk->bhts', qc, kc, decay)
        o = o_inter + jnp.einsum('bhts,bhsv->bhtv', attn, vc)
        last = cum[:, :, -1, :]
        k_dec = kc * jnp.exp(last[:, :, None, :] - cum)
        state = state * jnp.exp(last)[..., None] + jnp.einsum('bhsk,bhsv->bhkv', k_dec, vc)
        return state, o

    state0 = jnp.zeros((b, GLA_HEADS, GLA_DK, GLA_DV), jnp.float32)
    _, o = lax.scan(step, state0, (prep(q), prep(k), prep(v), prep(log_a)))
    return from_blocks(jnp.moveaxis(o, 2, 3))


def hybrid_mixer(h, w_in, w_out, lam_params, subln_g, w_a2, b_a, norm_g, lambda_init):
    b, s, _ = h.shape
    proj = h @ w_in
    idx = [int(i) for i in np.cumsum(HYB_SPLITS)[:-1]]
    dq, dk, dv, gq, gk, gv, gg, ga = jnp.split(proj, idx, axis=-1)
    a_out = diff_attention(dq.reshape(b, s, 2 * DIFF_HEADS, DIFF_DH),
                           dk.reshape(b, s, 2 * DIFF_HEADS, DIFF_DH),
                           dv.reshape(b, s, DIFF_HEADS, DIFF_DV),
                           lam_params, subln_g, lambda_init)
    f32 = jnp.float32
    log_a = jax.nn.log_sigmoid((ga @ w_a2 + b_a).astype(f32)) / GLA_TAU
    o_b = gla_chunked(gq.reshape(b, s, GLA_HEADS, GLA_DK).astype(f32) * (GLA_DK ** -0.5),
                      gk.reshape(b, s, GLA_HEADS, GLA_DK).astype(f32),
                      gv.reshape(b, s, GLA_HEADS, GLA_DV).astype(f32),
                      log_a.reshape(b, s, GLA_HEADS, GLA_DK))
    o_b = rmsnorm(o_b, norm_g) * jax.nn.silu(gg.reshape(b, s, GLA_HEADS, GLA_DV).astype(f32))
    b_out = o_b.reshape(b, s, GLA_HEADS * GLA_DV).astype(h.dtype)
    return jnp.concatenate([a_out, b_out], axis=-1) @ w_out


def stick_breaking(h, w_qkv, w_out):
    b, s, _ = h.shape
    f32 = jnp.float32
    q, k, v = jnp.split(h @ w_qkv, 3, axis=-1)
    qf = q.reshape(b, s, SB_HEADS, SB_DH).astype(f32) * (SB_DH ** -0.5)
    kf = k.reshape(b, s, SB_HEADS, SB_DH).astype(f32)
    vf = v.reshape(b, s, SB_HEADS, SB_DH).astype(f32)
    within_tri = strict_upper(Q_BLOCK, f32)
    outs = []
    for i in range(s // Q_BLOCK):
        start, end = i * Q_BLOCK, (i + 1) * Q_BLOCK
        nk = end // Q_BLOCK
        z = jnp.einsum('bqhd,bkhd->bhqk', qf[:, start:end], kf[:, :end])
        mask = jnp.arange(end)[None, :] < (start + jnp.arange(Q_BLOCK))[:, None]
        log_keep = jnp.where(mask, -jax.nn.softplus(z), 0.0)
        lk = log_keep.reshape(b, SB_HEADS, Q_BLOCK, nk, Q_BLOCK)
        within = jnp.einsum('bhqnj,js->bhqns', lk, within_tri)
        later = jnp.einsum('bhqm,mn->bhqn', jnp.sum(lk, axis=-1), strict_upper(nk, f32))
        suffix = (within + later[..., None]).reshape(b, SB_HEADS, Q_BLOCK, end)
        att = jnp.where(mask, jnp.exp(z + log_keep + suffix), 0.0)
        outs.append(jnp.einsum('bhqk,bkhd->bqhd', att, vf[:, :end]))
    o = jnp.concatenate(outs, axis=1)
    return o.reshape(b, s, D_MODEL).astype(h.dtype) @ w_out


def setup_inputs(seed: int = 0) -> dict:
    key = jax.random.key(seed)
    ks = jax.random.split(key, 20)
    f32 = jnp.float32

    def nrm(k, shape, fan_in):
        return jax.random.normal(k, shape, f32) * (fan_in ** -0.5)

    def gain(k, shape):
        return 1.0 + 0.02 * jax.random.normal(k, shape, f32)

    return {
        "x": jax.random.normal(ks[0], (BATCH, SEQ, D_MODEL), f32),
        "ffn_pre_g": gain(ks[1], (DEPTH, N_FFN, D_MODEL)),
        "ffn_post_g": gain(ks[2], (DEPTH, N_FFN, D_MODEL)),
        "ffn_w_gate": nrm(ks[3], (DEPTH, N_FFN, D_MODEL, D_FF), D_MODEL),
        "ffn_w_up": nrm(ks[4], (DEPTH, N_FFN, D_MODEL, D_FF), D_MODEL),
        "ffn_w_down": nrm(ks[5], (DEPTH, N_FFN, D_FF, D_MODEL), D_FF),
        "mix_pre_g": gain(ks[6], (DEPTH, D_MODEL)),
        "mix_post_g": gain(ks[7], (DEPTH, D_MODEL)),
        "hyb_w_in": nrm(ks[8], (N_EVEN, D_MODEL, HYB_IN), D_MODEL),
        "hyb_w_out": nrm(ks[9], (N_EVEN, HYB_OUT, D_MODEL), HYB_OUT),
        "diff_lambda": 0.1 * jax.random.normal(ks[10], (N_EVEN, 4, DIFF_DH), f32),
        "diff_subln_g": gain(ks[11], (N_EVEN, DIFF_DV)),
        "gla_w_a2": nrm(ks[12], (N_EVEN, GLA_RANK, GLA_HEADS * GLA_DK), GLA_RANK),
        "gla_b_a": 0.1 * jax.random.normal(ks[13], (N_EVEN, GLA_HEADS * GLA_DK), f32),
        "gla_norm_g": gain(ks[14], (N_EVEN, GLA_DV)),
        "sb_w_qkv": nrm(ks[15], (N_ODD, D_MODEL, 3 * D_MODEL), D_MODEL),
        "sb_w_out": nrm(ks[16], (N_ODD, D_MODEL, D_MODEL), D_MODEL),
    }


def reference(x, ffn_pre_g, ffn_post_g, ffn_w_gate, ffn_w_up, ffn_w_down, mix_pre_g, mix_post_g,
              hyb_w_in, hyb_w_out, diff_lambda, diff_subln_g, gla_w_a2, gla_b_a, gla_norm_g,
              sb_w_qkv, sb_w_out):
    h = x
    for layer in range(DEPTH):
        f = swiglu(rmsnorm(h, ffn_pre_g[layer, 0]), ffn_w_gate[layer, 0], ffn_w_up[layer, 0], ffn_w_down[layer, 0])
        h = h + 0.5 * rmsnorm(f, ffn_post_g[layer, 0])
        m = rmsnorm(h, mix_pre_g[layer])
        if layer % 2 == 0:
            e = layer // 2
            lambda_init = 0.8 - 0.6 * math.exp(-0.3 * layer)
            m = hybrid_mixer(m, hyb_w_in[e], hyb_w_out[e], diff_lambda[e], diff_subln_g[e],
                             gla_w_a2[e], gla_b_a[e], gla_norm_g[e], lambda_init)
        else:
            o = layer // 2
            m = stick_breaking(m, sb_w_qkv[o], sb_w_out[o])
        h = h + rmsnorm(m, mix_post_g[layer])
        f = swiglu(rmsnorm(h, ffn_pre_g[layer, 1]), ffn_w_gate[layer, 1], ffn_w_up[layer, 1], ffn_w_down[layer, 1])
        h = h + 0.5 * rmsnorm(f, ffn_post_g[layer, 1])
    return h
```

```python
import math
from contextlib import ExitStack

import numpy as np
import ml_dtypes

import concourse.bass as bass
import concourse.mybir as mybir
from concourse.bass_utils import run_bass_kernel_spmd

F32 = mybir.dt.float32
BF16 = mybir.dt.bfloat16
AF = mybir.ActivationFunctionType
ALU = mybir.AluOpType
AX = mybir.AxisListType

D = 1024
DFF = 1408
NFC = DFF // 128
EPS = 1e-6
NEG = -30000.0
HYB_IN = 3088


class Buf:
    __slots__ = ("name", "w", "r", "pw", "pr")

    def __init__(self, name=""):
        self.name = name
        self.w = []
        self.r = []
        self.pw = []
        self.pr = []


class Op:
    __slots__ = ("eng", "fn", "deps", "dma", "sem", "val", "signal", "idx", "dinc")

    def __init__(self, eng, fn, dma):
        self.eng = eng
        self.fn = fn
        self.dma = dma
        self.deps = []
        self.sem = None
        self.val = 0
        self.signal = False
        self.idx = 0
        self.dinc = 16


ENGS = ("pe", "act", "dve", "pool", "sp")
N_DMA_SEMS = 80


class Prog:
    def __init__(self, nc, same_engine_sync=True):
        self.nc = nc
        self.ops = {e: [] for e in ENGS}
        self.all = []
        self.same_engine_sync = same_engine_sync
        self.bar_from = 0

    def op(self, eng, fn, reads=(), writes=(), dma=None, par=False):
        o = Op(eng, fn, dma)
        o.idx = len(self.all)
        self.all.append(o)
        deps = {}
        for b in reads:
            for w in b.w:
                deps[id(w)] = w
        for b in writes:
            if b.r or (not par) or (not b.w):
                b.pw = b.w
                b.pr = b.r
                b.w = []
                b.r = []
            for x in b.pw:
                deps[id(x)] = x
            for x in b.pr:
                deps[id(x)] = x
            b.w.append(o)
        for b in reads:
            if b not in writes:
                b.r.append(o)
        for d in deps.values():
            if d is o:
                continue
            if d.dma is None and d.eng == eng:
                if eng == "pe" or not self.same_engine_sync:
                    continue
            o.deps.append(d)
            d.signal = True
        self.ops[eng].append(o)
        return o

    def cc_allgather(self, out, in_, groups, reads=(), writes=(), sem=None):
        o = self.op("pool", lambda e: e.collective_compute(
            "AllGather", ALU.bypass, replica_groups=groups, ins=[in_], outs=[out]),
            reads, writes, dma=sem, par=True)
        o.dinc = 1
        return o

    def barrier(self):
        lasts = []
        for e in ENGS:
            for o in reversed(self.ops[e]):
                if o.fn is not None and o.dma is None:
                    lasts.append(o)
                    break
        dmas = [o for o in self.all[self.bar_from:] if o.dma is not None and o.dinc == 16]
        self.bar_from = len(self.all)
        for e in ENGS:
            o = Op(e, None, None)
            o.idx = len(self.all)
            self.all.append(o)
            o.deps = [d for d in lasts if d.eng != e] + dmas
            for d in o.deps:
                d.signal = True
            self.ops[e].append(o)

    def dma(self, q, out, in_, reads=(), writes=(), sem=None, par=False, **kw):
        return self.op(q, lambda e: e.dma_start(out=out, in_=in_, **kw), reads, writes,
                       dma=sem, par=par)

    def finalize(self, stack):
        nc = self.nc
        fin = Op("sp", None, None)
        fin.idx = len(self.all)
        fin.deps = [o for o in self.all if o.dma is not None]
        self.all.append(fin)
        self.ops["sp"].append(fin)
        esem = {e: stack.enter_context(nc.semaphore("s_" + e)) for e in ENGS}
        cnt = {e: 0 for e in ENGS}
        last_use = {}
        for o in self.all:
            if o.dma is not None:
                last_use[id(o.dma)] = o.idx
        pool = [stack.enter_context(nc.semaphore("d_%d" % i)) for i in range(N_DMA_SEMS)]
        pcount = [0] * N_DMA_SEMS
        free = list(range(N_DMA_SEMS))
        owner = {}
        release_at = {}
        for o in self.all:
            if o.dma is not None:
                k = id(o.dma)
                if k not in owner:
                    assert free, "out of DMA semaphores"
                    owner[k] = free.pop(0)
                si = owner[k]
                pcount[si] += o.dinc
                o.sem = pool[si]
                o.val = pcount[si]
                if last_use[k] == o.idx:
                    free.append(si)
                    del owner[k]
            elif o.signal:
                cnt[o.eng] += 1
                o.sem = esem[o.eng]
                o.val = cnt[o.eng]
        self.counts = dict(cnt)

        def replay(ename, eng):
            waited = {}
            for o in self.ops[ename]:
                need = {}
                for d in o.deps:
                    k = id(d.sem)
                    if waited.get(k, 0) >= d.val:
                        continue
                    if k not in need or need[k][1] < d.val:
                        need[k] = (d.sem, d.val)
                for k, (s, v) in need.items():
                    eng.wait_ge(s, v)
                    waited[k] = v
                if o.fn is None:
                    continue
                ins = o.fn(eng)
                if o.dma is not None:
                    ins.then_inc(o.sem, o.dinc)
                elif o.signal:
                    ins.then_inc(o.sem, 1)

        with nc.Block() as block:
            @block.sync
            def _(eng):
                replay("sp", eng)

            @block.tensor
            def _(eng):
                replay("pe", eng)

            @block.scalar
            def _(eng):
                replay("act", eng)

            @block.vector
            def _(eng):
                replay("dve", eng)

            @block.gpsimd
            def _(eng):
                replay("pool", eng)


class Ctx:
    def __init__(self, same_engine_sync=True):
        self.nc = bass.Bass("TRN2", target_bir_lowering=False)
        self.P = Prog(self.nc, same_engine_sync)
        self.stack = ExitStack()
        self.names = 0

    def sb(self, shape, dt, name=None, stack=None):
        self.names += 1
        t = (stack or self.stack).enter_context(
            self.nc.sbuf_tensor("%s_%d" % (name or "sb", self.names), list(shape), dt))
        return t

    def ps(self, shape, dt=F32, name=None, stack=None):
        self.names += 1
        t = (stack or self.stack).enter_context(
            self.nc.psum_tensor("%s_%d" % (name or "ps", self.names), list(shape), dt))
        return t

    def dram(self, name, shape, dt, kind="Internal"):
        return self.nc.dram_tensor(name, list(shape), dt, kind=kind).ap()


def mm(P, out, lhsT, rhs, start, stop, reads, writes):
    return P.op("pe", lambda e: e.matmul(out, lhsT, rhs, start=start, stop=stop),
                reads, writes)


def _rd(a, b):
    return [a] if b is None else [a, b]


def _ib(in_b, key):
    return in_b[key] if isinstance(in_b, dict) else in_b


def mma(P, out, lhsT, rhs, reads, writes):
    return P.op("pe", lambda e: e.matmul(out, lhsT, rhs, start=False, stop=True,
                                         skip_group_check=True), reads, writes)


def tcopy(P, eng, out, in_, reads, writes, par=False):
    return P.op(eng, lambda e: e.tensor_copy(out=out, in_=in_), reads, writes, par=par)


def tt(P, eng, out, a, b, op, reads, writes, par=False):
    return P.op(eng, lambda e: e.tensor_tensor(out, a, b, op), reads, writes, par=par)


def ts(P, eng, out, a, s1, s2, op0, op1, reads, writes, par=False):
    if s2 is None:
        return P.op(eng, lambda e: e.tensor_scalar(out, a, s1, None, op0), reads, writes, par=par)
    return P.op(eng, lambda e: e.tensor_scalar(out, a, s1, s2, op0, op1), reads, writes, par=par)


def stt(P, out, a, s, b, op0, op1, reads, writes, par=False):
    return P.op("dve", lambda e: e.scalar_tensor_tensor(out, a, s, b, op0, op1), reads, writes,
                par=par)


def act(P, out, in_, func, reads, writes, **kw):
    return P.op("act", lambda e: e.activation(out=out, in_=in_, func=func, **kw),
                reads, writes)


def load_weight(cx, dst, dstbuf, src, KC, N, gcol=None, mul=None, stg=None, eng_rr=("dve", "pool")):
    P = cx.P
    stiles, sbufs, state = stg
    CH = stiles[0].shape[-1]
    for k in range(KC):
        for c0 in range(0, N, CH):
            cw = min(CH, N - c0)
            i = state[0] % len(stiles)
            state[0] += 1
            st, sbf = stiles[i], sbufs[i]
            P.dma("sp", st[:, 0:cw], src[:, k, c0:c0 + cw], writes=[sbf], sem=sbf)
            en = eng_rr[state[0] % len(eng_rr)]
            o_ap = dst[:, k, c0:c0 + cw]
            i_ap = st[:, 0:cw]
            if gcol is not None:
                s1 = gcol[0][:, k:k + 1]
                rd = [sbf, gcol[1]]
                if mul is not None:
                    P.op(en, lambda e, o=o_ap, i=i_ap, s=s1: e.tensor_scalar(
                        o, i, s, float(mul), ALU.mult, ALU.mult), rd, [dstbuf], par=True)
                else:
                    P.op(en, lambda e, o=o_ap, i=i_ap, s=s1: e.tensor_scalar(
                        o, i, s, None, ALU.mult), rd, [dstbuf], par=True)
            else:
                P.op(en, lambda e, o=o_ap, i=i_ap: e.tensor_copy(out=o, in_=i),
                     [sbf], [dstbuf], par=True)


def make_stage(cx, n=4, ch=1408, stack=None):
    tiles = [cx.sb([128, ch], F32, "wstg", stack) for _ in range(n)]
    bufs = [Buf("wstg%d" % i) for i in range(n)]
    return (tiles, bufs, [0])


def rstd_from_ss(P, rstd_ap, ss_ap, n, reads, writes, extra_mul=None):
    act(P, rstd_ap, ss_ap, AF.Ln, reads, writes, scale=1.0 / n, bias=EPS)
    b = 0.0 if extra_mul is None else math.log(extra_mul)
    act(P, rstd_ap, rstd_ap, AF.Exp, writes, writes, scale=-0.5, bias=b)


def load_weight_chunks(cx, dst, bufs, src, KC, N, CW, gcol=None, mul=None, stg=None,
                       eng_rr=("pool", "dve")):
    P = cx.P
    stiles, sbufs, state = stg
    for ci, c0 in enumerate(range(0, N, CW)):
        cw = min(CW, N - c0)
        i = state[0] % len(stiles)
        state[0] += 1
        st, sbf = stiles[i], sbufs[i]
        P.dma("sp", st[:, 0:KC, 0:cw], src[:, :, c0:c0 + cw], writes=[sbf], sem=sbf)
        for k in range(KC):
            en = eng_rr[(state[0] + k) % len(eng_rr)]
            o_ap = dst[:, k, c0:c0 + cw]
            i_ap = st[:, k, 0:cw]
            if gcol is not None:
                ts(P, en, o_ap, i_ap, gcol[0][:, k:k + 1], (None if mul is None else float(mul)),
                   ALU.mult, ALU.mult, [sbf, gcol[1]], [bufs[ci]], par=(k > 0))
            else:
                tcopy(P, en, o_ap, i_ap, [sbf], [bufs[ci]], par=(k > 0))


def make_stage3(cx, n, KC, CW, stack=None):
    tiles = [cx.sb([128, KC, CW], F32, "wstg3", stack) for _ in range(n)]
    bufs = [Buf("wstg3_%d" % i) for i in range(n)]
    return (tiles, bufs, [0])


class ChunkedW:
    def __init__(self, cx, st, name, KC, N, CW):
        self.cx, self.KC, self.N, self.CW = cx, KC, N, CW
        self.t = cx.sb([128, KC, N], BF16, name, st)
        self.nch = (N + CW - 1) // CW
        self.bufs = [Buf("%s_c%d" % (name, i)) for i in range(self.nch)]
        self.stg = make_stage3(cx, 3, KC, CW, st)

    def load(self, src_d, gcol, mul_of, order=None):
        for ci in (order if order is not None else range(self.nch)):
            c0 = ci * self.CW
            c1 = min(self.N, c0 + self.CW)
            load_weight_chunks(self.cx, self.t[:, :, c0:c1], [self.bufs[ci]], src_d[:, :, c0:c1],
                               self.KC, c1 - c0, self.CW, gcol=gcol, mul=mul_of(c0), stg=self.stg)

    def rd(self, c0, c1):
        return [self.bufs[i] for i in range(c0 // self.CW, (c1 - 1) // self.CW + 1)]


class NormPipe:
    def __init__(self, cx, C, st, h_in, h_in_b, ntp=2):
        self.cx, self.C, self.h_in, self.h_in_b = cx, C, h_in, h_in_b
        self.ntp = ntp
        self.hx = [cx.sb([128, 4, D], F32, "hx", st) for _ in range(2)]
        self.hx_b = [[Buf("hx") for _ in range(4)] for _ in range(2)]
        self.xnT = [cx.sb([128, 8, 512], BF16, "xnT", st) for _ in range(2)]
        self.xnT_b = [Buf("xnT") for _ in range(2)]
        self.xn = [cx.sb([128, D], BF16, "xn", st) for _ in range(4)]
        self.xn_b = [Buf("xn") for _ in range(4)]
        self.sm = [cx.sb([128, 8], F32, "nsm", st) for _ in range(8)]
        self.sm_b = [Buf("nsm") for _ in range(8)]
        self.junk = cx.sb([128, D], BF16, "njunk", st)
        self.tp = [cx.ps([128, 8, 128], BF16, "tp", st) for _ in range(ntp)]
        self.tp_b = [Buf("tp") for _ in range(ntp)]

    def loads(self, T):
        P = self.cx.P
        hb = T % 2
        for sb in range(4):
            ti = T * 4 + sb
            P.dma("sp", self.hx[hb][:, sb, :], self.h_in[ti * 128:(ti + 1) * 128, :],
                  reads=[self.h_in_b[ti]], writes=[self.hx_b[hb][sb]], sem=self.hx_b[hb][sb])

    def act_part(self, T):
        P = self.cx.P
        hb = T % 2
        for sb in range(4):
            s_ = self.sm[hb * 4 + sb]; s_b = self.sm_b[hb * 4 + sb]
            x_ap = self.hx[hb][:, sb, :]
            act(P, self.junk[:], x_ap, AF.Square, [self.hx_b[hb][sb]], [s_b], accum_out=s_[:, 0:1])
            rstd_from_ss(P, s_[:, 1:2], s_[:, 0:1], D, [s_b], [s_b])
            act(P, self.xn[sb][:], x_ap, AF.Copy, [self.hx_b[hb][sb], s_b], [self.xn_b[sb]],
                scale=s_[:, 1:2])

    def pe_part(self, T, sb):
        P = self.cx.P
        hb = T % 2
        i2 = sb % self.ntp
        tp, tp_b = self.tp[i2], self.tp_b[i2]
        xn = self.xn[sb]
        for k in range(8):
            P.op("pe", lambda e, k=k, tp=tp, xn=xn: e.transpose(
                tp[:, k, :], xn[:, k * 128:(k + 1) * 128], self.C.ident),
                [self.xn_b[sb], self.C.ident_b], [tp_b])
        tcopy(P, "dve", self.xnT[hb][:, :, sb * 128:(sb + 1) * 128], tp[:], [tp_b],
              [self.xnT_b[hb]], par=(sb > 0))


class Consts:
    pass


def load_consts(cx, cst_dram, msk_dram):
    P = cx.P
    C = Consts()
    C.cst = cx.sb([128, 8, 128], BF16, "cst")
    C.b = Buf("cst")
    P.dma("sp", C.cst[:], cst_dram, writes=[C.b], sem=C.b)
    C.msk = cx.sb([128, 2, 4, 128], BF16, "msk")
    P.dma("sp", C.msk[:], msk_dram, writes=[C.b], sem=C.b, par=True)
    C.ident = C.cst[:, 0, :]
    C.ntri = C.cst[:, 1, :]
    C.nones = C.cst[:, 2, :]
    C.triu01 = C.cst[:, 3, :]
    C.triu01x4 = C.cst[:, 4:8, :]
    C.ident_b = C.b
    return C


def load_consts_f32(cx, C, cstf_dram, stack=None):
    C.cstf = cx.sb([128, 2, 128], F32, "cstf", stack)
    C.fb = Buf("cstf")
    cx.P.dma("sp", C.cstf[:], cstf_dram, writes=[C.fb], sem=C.fb)
    C.triu01f = C.cstf[:, 0, :]
    C.onesf = C.cstf[:, 1, :]


def host_consts_f32():
    i = np.arange(128)
    c = np.zeros((128, 2, 128), np.float32)
    c[:, 0] = (i[:, None] <= i[None, :])
    c[:, 1] = 1.0
    return c


def host_consts():
    i = np.arange(128)
    cst = np.zeros((128, 8, 128), np.float32)
    cst[:, 0] = np.eye(128)
    cst[:, 1] = -(i[:, None] >= i[None, :]).astype(np.float32)
    cst[:, 2] = -1.0
    cst[:, 3] = (i[:, None] <= i[None, :]).astype(np.float32)
    for k in range(4, 8):
        cst[:, k] = cst[:, 3]
    return cst.astype(ml_dtypes.bfloat16)


def host_masks(r):
    i = np.arange(128)
    m = np.zeros((128, 2, 4, 128), np.float32)
    for rp in range(4):
        if rp > r:
            m[:, :, rp] = NEG
        elif rp == r:
            m[:, 0, rp] = np.where(i[:, None] >= i[None, :], NEG, 0.0)
            m[:, 1, rp] = np.where(i[:, None] > i[None, :], NEG, 0.0)
    return m.astype(ml_dtypes.bfloat16)


def norm_transpose(cx, C, x_ap, xbuf, xnT, xnT_buf, col0, ss, rstd, small_b, junk, junk_b,
                   xn, xn_b, tp, tp_b):
    P = cx.P
    act(P, junk[:], x_ap, AF.Square, [xbuf], [small_b], accum_out=ss)
    rstd_from_ss(P, rstd, ss, D, [small_b], [small_b])
    act(P, xn[:], x_ap, AF.Copy, [xbuf, small_b], [xn_b], scale=rstd)
    for k in range(8):
        P.op("pe", lambda e, k=k: e.transpose(tp[:, k, :], xn[:, k * 128:(k + 1) * 128],
                                              C.ident),
             [xn_b, C.ident_b], [tp_b])
    P.op("dve", lambda e: e.tensor_copy(out=xnT[:, :, col0:col0 + 128], in_=tp[:]),
         [tp_b], [xnT_buf], par=True)


def resid_tail(P, s_, s_b, po, po_b, gpost, gpost_b, tmp, tmp_b, hx_ap, hx_b, h_out_ap,
               h_out_b, mul):
    P.op("dve", lambda e: e.tensor_tensor(s_[:, 2:3], s_[:, 0:1], s_[:, 1:2], ALU.add),
         [s_b], [s_b])
    rstd_from_ss(P, s_[:, 3:4], s_[:, 2:3], D, [s_b], [s_b],
                 extra_mul=(None if mul == 1.0 else mul))
    for half in range(2):
        P.op("dve", lambda e, half=half: e.tensor_tensor(
            tmp[:, half * 512:(half + 1) * 512], po[half][:],
            gpost[:, half * 512:(half + 1) * 512], ALU.mult),
            [po_b[half], gpost_b], [tmp_b], par=(half == 1))
    P.op("dve", lambda e: e.scalar_tensor_tensor(
        tmp[:], tmp[:], s_[:, 3:4], hx_ap, ALU.mult, ALU.add),
        [tmp_b, s_b, hx_b], [tmp_b])
    P.dma("sp", h_out_ap, tmp[:], reads=[tmp_b], writes=[h_out_b], sem=tmp_b)


def ffn_stage(cx, C, h_in, h_in_b, h_out, h_out_b, TL, wg_d, wu_d, wd_d, pre_gT_d, post_g_d):
    P = cx.P
    NT = TL // 512
    with ExitStack() as st:
        P.barrier()
        wg = cx.sb([128, 8, DFF], BF16, "wg", st); wg_b = [Buf("wg") for _ in range(NFC)]
        wu = cx.sb([128, 8, DFF], BF16, "wu", st); wu_b = [Buf("wu") for _ in range(NFC)]
        wd = cx.sb([128, NFC, D], BF16, "wd", st); wd_b = [Buf("wd") for _ in range(4)]
        pg = cx.sb([128, 8], F32, "pre_g", st); pg_b = Buf("pre_g")
        gpost = cx.sb([128, D], F32, "gpost", st); gpost_b = Buf("gpost")
        stg = make_stage3(cx, 3, NFC, 256, st)
        np_ = NormPipe(cx, C, st, h_in, h_in_b, ntp=1)
        np_.loads(0)
        P.dma("sp", pg[:], pre_gT_d, writes=[pg_b], sem=pg_b)
        P.dma("sp", gpost[:], post_g_d, writes=[gpost_b], sem=gpost_b)
        for fc in range(NFC):
            c0 = fc * 128
            for (w_, wb_, src_) in ((wg, wg_b, wg_d), (wu, wu_b, wu_d)):
                load_weight_chunks(cx, w_[:, :, c0:c0 + 128], [wb_[fc]], src_[:, :, c0:c0 + 128],
                                   8, 128, 128, gcol=(pg, pg_b), stg=stg)
        for q in range(4):
            load_weight_chunks(cx, wd[:, :, q * 256:(q + 1) * 256], [wd_b[q]],
                               wd_d[:, :, q * 256:(q + 1) * 256], NFC, 256, 256, stg=stg)

        fT = cx.sb([128, NFC, 512], BF16, "fT", st); fT_b = [Buf("fT") for _ in range(NFC)]
        junk = cx.sb([128, 512], BF16, "junk", st)
        small = [cx.sb([128, 8], F32, "small", st) for _ in range(4)]
        small_b = [Buf("small") for _ in range(4)]
        sg = [cx.sb([128, 512], F32, "sg", st) for _ in range(2)]; sg_b = [Buf("sg") for _ in range(2)]
        tmp = [cx.sb([128, D], F32, "tmp", st) for _ in range(2)]; tmp_b = [Buf("tmp") for _ in range(2)]
        pgate = [cx.ps([128, 512], F32, "pgate", st) for _ in range(2)]; pgate_b = [Buf() for _ in range(2)]
        pup = [cx.ps([128, 512], F32, "pup", st) for _ in range(2)]; pup_b = [Buf() for _ in range(2)]
        po3 = [cx.ps([128, 512], F32, "po", st) for _ in range(3)]; po3_b = [Buf() for _ in range(3)]
        pcount = 0

        np_.act_part(0)
        for sb in range(4):
            np_.pe_part(0, sb)
        sidx = 0
        for T in range(NT):
            hb = T % 2
            xT, xT_b = np_.xnT[hb], np_.xnT_b[hb]
            if T + 1 < NT:
                np_.loads(T + 1)
                np_.act_part(T + 1)
            for fc in range(NFC):
                i2 = fc % 2
                for k in range(8):
                    mm(P, pgate[i2][:], wg[:, k, fc * 128:(fc + 1) * 128], xT[:, k, :],
                       k == 0, k == 7, [wg_b[fc], xT_b], [pgate_b[i2]])
                for k in range(8):
                    mm(P, pup[i2][:], wu[:, k, fc * 128:(fc + 1) * 128], xT[:, k, :],
                       k == 0, k == 7, [wu_b[fc], xT_b], [pup_b[i2]])
                act(P, sg[i2][:], pgate[i2][:], AF.Silu, [pgate_b[i2]], [sg_b[i2]])
                tt(P, "dve", fT[:, fc, :], sg[i2][:], pup[i2][:], ALU.mult,
                   [sg_b[i2], pup_b[i2]], [fT_b[fc]])
                if T + 1 < NT and fc in (2, 4, 6, 8):
                    np_.pe_part(T + 1, (fc - 2) // 2)
            for sb in range(4):
                ti = T * 4 + sb
                s_ = small[sidx % 4]; s_b = small_b[sidx % 4]; sidx += 1
                i2 = sb % 2
                bk = [pcount % 3, (pcount + 1) % 3]; pcount += 2
                po = [po3[bk[0]], po3[bk[1]]]; po_b = [po3_b[bk[0]], po3_b[bk[1]]]
                for half in range(2):
                    for fc in range(NFC):
                        mm(P, po[half][:], fT[:, fc, sb * 128:(sb + 1) * 128],
                           wd[:, fc, half * 512:(half + 1) * 512], fc == 0, fc == NFC - 1,
                           [fT_b[fc], wd_b[2 * half], wd_b[2 * half + 1]], [po_b[half]])
                    act(P, junk[:], po[half][:], AF.Square, [po_b[half]],
                        [s_b], accum_out=s_[:, half:half + 1])
                resid_tail(P, s_, s_b, po, po_b, gpost, gpost_b, tmp[i2], tmp_b[i2],
                           np_.hx[hb][:, sb, :], np_.hx_b[hb][sb],
                           h_out[ti * 128:(ti + 1) * 128, :], h_out_b[ti], 0.5)


def sb_proj_stage(cx, C, h_in, h_in_b, TL, wqkv_d, pre_gT_d, qT_d, kT_d, v_d, out_b):
    P = cx.P
    NT = TL // 512
    with ExitStack() as st:
        P.barrier()
        W = ChunkedW(cx, st, "wqkv", 8, 3 * D, 256)
        w = W.t
        pg = cx.sb([128, 8], F32, "pre_g", st); pg_b = Buf("pre_g")
        np_ = NormPipe(cx, C, st, h_in, h_in_b)
        np_.loads(0)
        P.dma("sp", pg[:], pre_gT_d, writes=[pg_b], sem=pg_b)
        W.load(wqkv_d, (pg, pg_b), lambda c0: (128 ** -0.5 if c0 < D else None))
        qkst = [cx.sb([128, 8, 512], BF16, "qkst", st) for _ in range(2)]
        qkst_b = [Buf("qkst") for _ in range(2)]
        vst = [cx.sb([128, 8, 4, 128], BF16, "vst", st) for _ in range(2)]
        vst_b = [Buf("vst") for _ in range(2)]
        pp = [cx.ps([128, 512], F32, "pp", st) for _ in range(4)]; pp_b = [Buf() for _ in range(4)]
        pi = 0
        qi = 0
        np_.act_part(0)
        for sb in range(4):
            np_.pe_part(0, sb)
        for T in range(NT):
            hb = T % 2
            xnT, xnT_b = np_.xnT, np_.xnT_b
            if T + 1 < NT:
                np_.loads(T + 1)
                np_.act_part(T + 1)
            for which, dst in ((0, qT_d), (1, kT_d)):
                qb = qi % 2; qi += 1
                for h in range(8):
                    p_ = pp[pi % 4]; p_b = pp_b[pi % 4]; pi += 1
                    c0 = which * D + h * 128
                    for k in range(8):
                        mm(P, p_[:], w[:, k, c0:c0 + 128], xnT[hb][:, k, :], k == 0, k == 7,
                           W.rd(c0, c0 + 128) + [xnT_b[hb]], [p_b])
                    if T + 1 < NT and which == 1 and h in (0, 2, 4, 6):
                        np_.pe_part(T + 1, h // 2)
                    if h % 2 == 0:
                        act(P, qkst[qb][:, h, :], p_[:], AF.Copy, [p_b], [qkst_b[qb]])
                    else:
                        P.op("dve", lambda e, qb=qb, h=h, p_=p_: e.tensor_copy(
                            out=qkst[qb][:, h, :], in_=p_[:]), [p_b], [qkst_b[qb]],
                            par=(h > 1))
                P.dma("sp", dst[:, :, T * 512:(T + 1) * 512].rearrange("h d t -> d h t"),
                      qkst[qb][:], reads=[qkst_b[qb]], writes=[out_b], sem=qkst_b[qb], par=True)
            vb = T % 2
            for sb in range(4):
                for half in range(2):
                    p_ = pp[pi % 4]; p_b = pp_b[pi % 4]; pi += 1
                    for k in range(8):
                        mm(P, p_[:], xnT[hb][:, k, sb * 128:(sb + 1) * 128],
                           w[:, k, 2 * D + half * 512:2 * D + (half + 1) * 512], k == 0, k == 7,
                           W.rd(2 * D + half * 512, 2 * D + (half + 1) * 512) + [xnT_b[hb]], [p_b])
                    o_ap = vst[vb][:, half * 4:(half + 1) * 4, sb, :]
                    i_ap = p_[:].rearrange("p (h d) -> p h d", h=4)
                    if half == 0:
                        act(P, o_ap, i_ap, AF.Copy, [p_b], [vst_b[vb]])
                    else:
                        P.op("dve", lambda e, o_ap=o_ap, i_ap=i_ap: e.tensor_copy(
                            out=o_ap, in_=i_ap), [p_b], [vst_b[vb]], par=True)
            P.dma("sp", v_d[:, :, T * 4:(T + 1) * 4, :].rearrange("h p j d -> p h j d"),
                  vst[vb][:], reads=[vst_b[vb]], writes=[out_b], sem=vst_b[vb], par=True)


def sb_attn_stage(cx, C, TL, qT_d, kTg_d, vg_d, in_b, oT_d, out_b, also_reads=None):
    P = cx.P
    NJ = TL // 128
    NG = NJ // 4
    with ExitStack() as st:
        P.barrier()
        KT = [cx.sb([128, 4, TL], BF16, "KT", st) for _ in range(2)]
        V = [cx.sb([128, 4, NJ, 128], BF16, "V", st) for _ in range(2)]
        QT = [cx.sb([128, TL], BF16, "QT", st) for _ in range(2)]
        kv_b = [Buf("kv") for _ in range(2)]
        OT = [cx.sb([128, TL], BF16, "OT", st) for _ in range(2)]
        OT_b = [Buf("OT") for _ in range(2)]
        e_t = [cx.sb([128, 1024], F32, "e", st) for _ in range(3)]; e_b = [Buf("e") for _ in range(3)]
        a_t = [cx.sb([128, 1024], F32, "a", st) for _ in range(2)]; a_b = [Buf("a") for _ in range(2)]
        sp_t = [cx.sb([128, 1024], BF16, "sp", st) for _ in range(2)]; sp_b = [Buf("sp") for _ in range(2)]
        at_t = [cx.sb([128, 1024], BF16, "att", st) for _ in range(2)]; at_b = [Buf("att") for _ in range(2)]
        sf_t = [cx.sb([128, 512], BF16, "sbf", st) for _ in range(2)]; sf_b = [Buf("sbf") for _ in range(2)]
        Z = cx.ps([128, 1024], F32, "Z", st); Z_b = Buf("Z")
        R = cx.ps([128, 1024], F32, "R", st); R_b = Buf("R")
        S = [cx.ps([128, 512], F32, "S", st) for _ in range(2)]; S_b = [Buf("S") for _ in range(2)]
        O = [cx.ps([128, 512], F32, "O", st) for _ in range(2)]; O_b = [Buf("O") for _ in range(2)]

        def load_head(h):
            i = h % 2
            P.dma("sp", KT[i][:], kTg_d[h].rearrange("r d t -> d r t"),
                  reads=_rd(_ib(in_b, h), also_reads), writes=[kv_b[i]], sem=kv_b[i])
            P.dma("sp", V[i][:], vg_d[h].rearrange("r p j d -> p r j d"),
                  reads=_rd(_ib(in_b, h), also_reads), writes=[kv_b[i]], sem=kv_b[i], par=True)
            P.dma("sp", QT[i][:], qT_d[h, :, :], reads=_rd(_ib(in_b, h), also_reads),
                  writes=[kv_b[i]], sem=kv_b[i], par=True)

        load_head(0)
        sidx = 0
        gidx = 0
        for h in range(8):
            hi = h % 2
            if h + 1 < 8:
                load_head(h + 1)
            kt, v, qt, kvb = KT[hi], V[hi], QT[hi], kv_b[hi]
            for G in range(NG):
                gi = gidx % 2; gidx += 1
                q0 = G * 512
                units = [(rp, jp) for jp in range(4 * G + 3, -1, -1) for rp in (3, 2, 1, 0)]
                steps = [[u] for u in units[:16]]
                rest = units[16:]
                steps += [rest[i:i + 2] for i in range(0, len(rest), 2)]

                def c0_of(u):
                    return max(0, u[1] - 4 * G) * 128

                def span(step):
                    if len(step) == 2:
                        return 0, 1024
                    return c0_of(step[0]), 512

                def stage_z(step):
                    for i, (rp, jp) in enumerate(step):
                        diag = jp >= 4 * G
                        c0 = c0_of((rp, jp))
                        o = i * 512
                        lhs = kt[:, rp, jp * 128:(jp + 1) * 128]
                        if diag:
                            mm(P, Z[:, c0:c0 + 128], lhs, qt[:, q0 + c0:q0 + c0 + 128], True, False,
                               [kvb], [Z_b])
                            mm(P, Z[:, c0:c0 + 128], C.ident, C.msk[:, 0, rp, :], False, True,
                               [C.b], [Z_b])
                            if c0 + 128 < 512:
                                mm(P, Z[:, c0 + 128:512], lhs, qt[:, q0 + c0 + 128:q0 + 512], True, True,
                                   [kvb], [Z_b])
                        else:
                            mm(P, Z[:, o:o + 512], lhs, qt[:, q0:q0 + 512], True, True, [kvb], [Z_b])

                def stage_e(step, si):
                    lo, hi = span(step)
                    ei = si % 3
                    act(P, e_t[ei][:, lo:hi], Z[:, lo:hi], AF.Exp, [Z_b], [e_b[ei]])

                def stage_l(step, si):
                    lo, hi = span(step)
                    act(P, sp_t[si % 2][:, lo:hi], e_t[si % 3][:, lo:hi], AF.Ln, [e_b[si % 3]],
                        [sp_b[si % 2]], bias=1.0)

                def stage_r(step, si, first):
                    zi = si % 2
                    spt, spb = sp_t[zi], sp_b[zi]
                    sft, sfb = sf_t[si % 2], sf_b[si % 2]
                    for i, u in enumerate(step):
                        c0 = c0_of(u)
                        o = i * 512
                        new = (u[1] >= 4 * G) and u[0] == 3
                        rng = [(c0, c0 + 128, False), (c0 + 128, 512, True)] if new else [(c0, 512, True)]
                        for (a, b, has_s) in rng:
                            if a >= b:
                                continue
                            lst = [(C.ntri, spt[:, o + a:o + b], [C.b, spb])]
                            if has_s:
                                lst.append((C.nones, sft[:, a:b], [C.b, sfb]))
                                if i == 1:
                                    lst.append((C.nones, spt[:, a:b], [C.b, spb]))
                            for k, (l_, r_, rd) in enumerate(lst):
                                mm(P, R[:, o + a:o + b], l_, r_, k == 0, k == len(lst) - 1, rd, [R_b])

                def stage_s(step, si):
                    spt, spb = sp_t[si % 2], sp_b[si % 2]
                    for i, u in enumerate(step):
                        c0 = c0_of(u)
                        mma(P, S[gi][:, c0:], C.ident, spt[:, i * 512 + c0:(i + 1) * 512], [C.b, spb],
                            [S_b[gi]])
                    c0 = c0_of(step[0])
                    ni = (si + 1) % 2
                    tcopy(P, "dve", sf_t[ni][:, c0:], S[gi][:, c0:], [S_b[gi]], [sf_b[ni]])

                def stage_a(step, si):
                    lo, hi = span(step)
                    zi = si % 2
                    act(P, a_t[zi][:, lo:hi], R[:, lo:hi], AF.Exp, [R_b], [a_b[zi]])
                    tt(P, "dve", at_t[zi][:, lo:hi], e_t[si % 3][:, lo:hi], a_t[zi][:, lo:hi],
                       ALU.mult, [e_b[si % 3], a_b[zi]], [at_b[zi]])

                def stage_o(step, si):
                    zi = si % 2
                    for i, (rp, jp) in enumerate(step):
                        c0 = c0_of((rp, jp))
                        mma(P, O[gi][:, c0:], v[:, rp, jp, :], at_t[zi][:, i * 512 + c0:(i + 1) * 512],
                            [kvb, at_b[zi]], [O_b[gi]])

                ns = len(steps)
                base = sidx
                P.op("dve", lambda e, t=S[gi]: e.memset(t[:], 0.0), [], [S_b[gi]])
                P.op("dve", lambda e, t=O[gi]: e.memset(t[:], 0.0), [], [O_b[gi]])
                for i in range(-2, ns + 1):
                    if 0 <= i + 2 < ns:
                        stage_z(steps[i + 2])
                    if 0 <= i + 1 < ns:
                        stage_l(steps[i + 1], base + i + 1)
                    if 0 <= i - 1 < ns:
                        stage_o(steps[i - 1], base + i - 1)
                    if 0 <= i + 1 < ns:
                        stage_s(steps[i + 1], base + i + 1)
                    if 0 <= i + 2 < ns:
                        stage_e(steps[i + 2], base + i + 2)
                    if 0 <= i < ns:
                        stage_a(steps[i], base + i)
                    if 0 <= i + 1 < ns:
                        stage_r(steps[i + 1], base + i + 1, i + 1 == 0)
                sidx += ns
                P.op("dve", lambda e, gi=gi, hi=hi, q0=q0: e.tensor_copy(
                    out=OT[hi][:, q0:q0 + 512], in_=O[gi][:]), [O_b[gi]], [OT_b[hi]],
                    par=(G > 0))
            P.dma("sp", oT_d[h, :, :], OT[hi][:], reads=[OT_b[hi]], writes=[out_b],
                  sem=OT_b[hi], par=True)


def outproj_stage(cx, C, TL, fT_d, f_b, h_in, h_in_b, h_out, h_out_b, wout_d, post_g_d):
    P = cx.P
    NT = TL // 512
    with ExitStack() as st:
        P.barrier()
        w = cx.sb([128, 8, D], BF16, "wout", st); w_b = Buf("wout")
        gpost = cx.sb([128, D], F32, "gpost", st); gpost_b = Buf("gpost")
        stg = make_stage(cx, 4, 1024, st)
        P.dma("sp", gpost[:], post_g_d, writes=[gpost_b], sem=gpost_b)
        load_weight(cx, w, w_b, wout_d, 8, D, stg=stg)
        hx = [cx.sb([128, 4, D], F32, "hx", st) for _ in range(2)]
        hx_b = [[Buf("hx") for _ in range(4)] for _ in range(2)]
        ft = [cx.sb([128, 8, 512], BF16, "ft", st) for _ in range(2)]
        ft_b = [Buf("ft") for _ in range(2)]
        junk = cx.sb([128, 512], BF16, "junk", st)
        small = [cx.sb([128, 8], F32, "small", st) for _ in range(4)]
        small_b = [Buf("small") for _ in range(4)]
        tmp = [cx.sb([128, D], F32, "tmp", st) for _ in range(2)]; tmp_b = [Buf("tmp") for _ in range(2)]
        po4 = [cx.ps([128, 512], F32, "po", st) for _ in range(4)]; po4_b = [Buf() for _ in range(4)]
        sidx = 0
        for T in range(NT):
            hb = T % 2
            for sb in range(4):
                ti = T * 4 + sb
                P.dma("sp", hx[hb][:, sb, :], h_in[ti * 128:(ti + 1) * 128, :],
                      reads=[h_in_b[ti]], writes=[hx_b[hb][sb]], sem=hx_b[hb][sb])
            P.dma("sp", ft[hb][:], fT_d[:, :, T * 512:(T + 1) * 512].rearrange("c f t -> f c t"),
                  reads=[f_b], writes=[ft_b[hb]], sem=ft_b[hb])
            for sb in range(4):
                ti = T * 4 + sb
                s_ = small[sidx % 4]; s_b = small_b[sidx % 4]; sidx += 1
                i2 = sb % 2
                po = po4[2 * i2:2 * i2 + 2]; po_b = po4_b[2 * i2:2 * i2 + 2]
                for half in range(2):
                    for c in range(8):
                        mm(P, po[half][:], ft[hb][:, c, sb * 128:(sb + 1) * 128],
                           w[:, c, half * 512:(half + 1) * 512], c == 0, c == 7,
                           [ft_b[hb], w_b], [po_b[half]])
                    act(P, junk[:], po[half][:], AF.Square, [po_b[half]], [s_b],
                        accum_out=s_[:, half:half + 1])
                resid_tail(P, s_, s_b, po, po_b, gpost, gpost_b, tmp[i2], tmp_b[i2],
                           hx[hb][:, sb, :], hx_b[hb][sb], h_out[ti * 128:(ti + 1) * 128, :],
                           h_out_b[ti], 1.0)


import os
HYB_DBG = int(os.environ.get('HYB_DBG', '0'))
C_DQ, C_DK, C_DV, C_GQ, C_GK, C_GV, C_GG, C_GA = 0, 512, 1024, 1536, 1792, 2048, 2560, 3072


def hyb_proj_stage(cx, C, h_in, h_in_b, TL, win_d, pre_gT_d, wa2_d, ba_d, gnorm_d,
                   dqT_d, dkT_d, dv_d, qeT_d, oi_d, G_d, U_d, dd_d, out_b):
    P = cx.P
    NT = TL // 512
    NJ = TL // 128
    with ExitStack() as st:
        P.barrier()
        W = ChunkedW(cx, st, "win", 8, HYB_IN, 256)
        w = W.t
        pg = cx.sb([128, 8], F32, "pre_g", st); pg_b = Buf("pre_g")
        np_ = NormPipe(cx, C, st, h_in, h_in_b)
        np_.loads(0)
        P.dma("sp", pg[:], pre_gT_d, writes=[pg_b], sem=pg_b)
        W.load(win_d, (pg, pg_b),
               lambda c0: (0.125 if (c0 < C_DK or C_GQ <= c0 < C_GK) else None),
               order=[0, 1, 2, 3, 4, 5, 12, 6, 7, 8, 9, 10, 11])
        load_consts_f32(cx, C, cx.cstf_d, st)
        wa2 = cx.sb([16, 256], F32, "wa2", st)
        ba = cx.sb([1, 256], F32, "ba", st)
        gn = cx.sb([128, 512], F32, "gnorm", st)
        misc_b = Buf("misc")
        P.dma("sp", wa2[:], wa2_d, writes=[misc_b], sem=misc_b)
        P.dma("sp", ba[:], ba_d, writes=[misc_b], sem=misc_b, par=True)
        P.dma("sp", gn[:], gnorm_d, writes=[misc_b], sem=misc_b, par=True)

        qkst = [cx.sb([128, 4, 512], BF16, "qkst", st) for _ in range(2)]
        qkst_b = [Buf("qkst") for _ in range(2)]
        vst = [cx.sb([128, 4, 4, 130], BF16, "vst", st) for _ in range(2)]
        vst_b = [Buf("vst") for _ in range(2)]
        qest = [cx.sb([128, 2, 512], BF16, "qest", st) for _ in range(2)]
        qest_b = [Buf("qest") for _ in range(2)]
        dall = cx.sb([128, NJ, 2], F32, "dall", st); dall_b = Buf("dall")
        def two(shape, dt, nm):
            return [cx.sb(shape, dt, nm, st) for _ in range(2)], [Buf(nm) for _ in range(2)]
        gaT, gaT_b = two([16, 128], F32, "gaT")
        e_a, e_a_b = two([128, 256], F32, "e_a")
        sp_a, sp_a_b = two([128, 256], F32, "sp_a")
        ek, ek_b = two([128, 256], F32, "ek")
        ketok, ketok_b = two([128, 256], BF16, "ketok")
        eqT, eqT_b = two([128, 2, 128], F32, "eqT")
        ekT, ekT_b = two([128, 2, 128], F32, "ekT")
        keT, keT_b = two([128, 2, 128], BF16, "keT")
        qbd, qbd_b = two([128, 2, 2, 128], BF16, "qbd")
        for i in range(2):
            P.op("pool", lambda e, t=qbd[i]: e.memset(t[:], 0.0), [], [qbd_b[i]])
        vtok, vtok_b = two([128, 512], BF16, "vtok")
        gsil, gsil_b = two([128, 512], F32, "gsil")
        atb, atb_b = two([128, 4, 128], BF16, "atb")
        oisb, oisb_b = two([128, 512], F32, "oisb")
        u2, u2_b = two([128, 2, 128], F32, "u2")
        pp = [cx.ps([128, 512], F32, "pp", st) for _ in range(6)]; pp_b = [Buf() for _ in range(6)]
        for i in range(2):
            P.op("dve", lambda e, t=vst[i]: e.memset(t[:, :, :, 128:129], 1.0), [], [vst_b[i]])
            P.op("dve", lambda e, t=vst[i]: e.memset(t[:, :, :, 129:130], 0.0), [], [vst_b[i]],
                 par=True)
        st_ = {"pi": 0, "sidx": 0}

        def bank():
            i = st_["pi"] % 6
            st_["pi"] += 1
            return pp[i], pp_b[i]

        np_.act_part(0)
        for sb in range(4):
            np_.pe_part(0, sb)
        for T in range(NT):
            hb = T % 2
            xT, xT_b = np_.xnT[hb], np_.xnT_b[hb]
            if T + 1 < NT:
                np_.loads(T + 1)
                np_.act_part(T + 1)
            for which, dst in ((0, dqT_d), (1, dkT_d)):
                qb = (2 * T + which) % 2
                for h in range(4):
                    p_, p_b = bank()
                    c0 = (C_DQ if which == 0 else C_DK) + h * 128
                    for k in range(8):
                        mm(P, p_[:], w[:, k, c0:c0 + 128], xT[:, k, :], k == 0, k == 7,
                           W.rd(c0, c0 + 128) + [xT_b], [p_b])
                    if h % 2 == 0:
                        act(P, qkst[qb][:, h, :], p_[:], AF.Copy, [p_b], [qkst_b[qb]])
                    else:
                        tcopy(P, "dve", qkst[qb][:, h, :], p_[:], [p_b], [qkst_b[qb]], par=(h > 1))
                P.dma("sp", dst[:, :, T * 512:(T + 1) * 512].rearrange("h d t -> d h t"),
                      qkst[qb][:], reads=[qkst_b[qb]], writes=[out_b], sem=qkst_b[qb], par=True)
            vb = T % 2
            for sb in range(4):
                p_, p_b = bank()
                for k in range(8):
                    mm(P, p_[:], xT[:, k, sb * 128:(sb + 1) * 128], w[:, k, C_DV:C_DV + 512],
                       k == 0, k == 7, W.rd(C_DV, C_DV + 512) + [xT_b], [p_b])
                o_ap = vst[vb][:, :, sb, 0:128]
                i_ap = p_[:].rearrange("p (h d) -> p h d", h=4)
                if sb % 2 == 0:
                    act(P, o_ap, i_ap, AF.Copy, [p_b], [vst_b[vb]])
                else:
                    tcopy(P, "dve", o_ap, i_ap, [p_b], [vst_b[vb]], par=True)
            P.dma("sp", dv_d[:, :, T * 4:(T + 1) * 4, :].rearrange("h p j d -> p h j d"),
                  vst[vb][:], reads=[vst_b[vb]], writes=[out_b], sem=vst_b[vb], par=True)
            qeb = T % 2
            for sb in range(4):
                j = T * 4 + sb
                bi = j % 2
                cols = slice(sb * 128, (sb + 1) * 128)
                if HYB_DBG and HYB_DBG <= 1:
                    continue
                if T + 1 < NT:
                    np_.pe_part(T + 1, sb)
                p1, p1_b = bank()
                for k in range(8):
                    mm(P, p1[0:16, 0:128], w[:, k, C_GA:C_GA + 16], xT[:, k, cols], k == 0, k == 7,
                       W.rd(C_GA, C_GA + 16) + [xT_b], [p1_b])
                tcopy(P, "dve", gaT[bi][:], p1[0:16, 0:128], [p1_b], [gaT_b[bi]])
                p2, p2_b = bank()
                mm(P, p2[:, 0:256], gaT[bi][:], wa2[:], True, False, [gaT_b[bi], misc_b], [p2_b])
                mm(P, p2[:, 0:256], C.onesf[0:1, :], ba[:], False, True, [C.fb, misc_b], [p2_b])
                act(P, e_a[bi][:], p2[:, 0:256], AF.Exp, [p2_b], [e_a_b[bi]], scale=-1.0)
                act(P, sp_a[bi][:], e_a[bi][:], AF.Ln, [e_a_b[bi]], [sp_a_b[bi]], bias=1.0)
                if HYB_DBG and HYB_DBG <= 2:
                    continue
                p3, p3_b = bank()
                mm(P, p3[:, 0:256], C.triu01f, sp_a[bi][:], True, True, [C.fb, sp_a_b[bi]], [p3_b])
                for hp in range(2):
                    mm(P, p3[:, 256 + hp * 128:384 + hp * 128], sp_a[bi][:, hp * 128:(hp + 1) * 128],
                       C.triu01f, True, True, [C.fb, sp_a_b[bi]], [p3_b])
                act(P, ek[bi][:], p3[:, 0:256], AF.Exp, [p3_b], [ek_b[bi]], scale=1.0 / 16)
                p3T = p3[:, 256:512].rearrange("p (a t) -> p a t", a=2)
                act(P, eqT[bi][:], p3T, AF.Exp, [p3_b], [eqT_b[bi]], scale=-1.0 / 16)
                act(P, ekT[bi][:], p3T, AF.Exp, [p3_b], [ekT_b[bi]], scale=1.0 / 16)
                if HYB_DBG and HYB_DBG <= 3:
                    continue
                p4, p4_b = bank()
                for k in range(8):
                    mm(P, p4[:, 0:256], xT[:, k, cols], w[:, k, C_GK:C_GK + 256], k == 0, k == 7,
                       W.rd(C_GK, C_GK + 256) + [xT_b], [p4_b])
                tt(P, "dve", ketok[bi][:], p4[:, 0:256], ek[bi][:], ALU.mult, [p4_b, ek_b[bi]],
                   [ketok_b[bi]])
                p5, p5_b = bank()
                for hp in range(2):
                    for which, cb in ((0, C_GQ), (1, C_GK)):
                        o_ = p5[:, (2 * which + hp) * 128:(2 * which + hp + 1) * 128]
                        for k in range(8):
                            mm(P, o_, w[:, k, cb + hp * 128:cb + (hp + 1) * 128], xT[:, k, cols],
                               k == 0, k == 7, W.rd(cb, cb + 256) + [xT_b], [p5_b])
                p5v = p5[:].rearrange("p (a t) -> p a t", a=4)
                tt(P, "dve", qest[qeb][:, :, cols], p5v[:, 0:2, :], eqT[bi][:], ALU.mult,
                   [p5_b, eqT_b[bi]], [qest_b[qeb]], par=(sb > 0))
                tt(P, "dve", keT[bi][:], p5v[:, 2:4, :], ekT[bi][:], ALU.mult,
                   [p5_b, ekT_b[bi]], [keT_b[bi]])
                for hh in range(2):
                    rows = slice(hh * 64, (hh + 1) * 64)
                    tt(P, "dve", qbd[bi][rows, :, hh, :], p5v[rows, 0:2, :], eqT[bi][rows, :, :],
                       ALU.mult, [p5_b, eqT_b[bi]], [qbd_b[bi]], par=(hh > 0))
                tcopy(P, "dve", dall[:, j, :], eqT[bi][:, :, 127], [eqT_b[bi]], [dall_b], par=True)
                if HYB_DBG and HYB_DBG <= 4:
                    continue
                p6, p6_b = bank()
                for k in range(8):
                    mm(P, p6[:], xT[:, k, cols], w[:, k, C_GV:C_GV + 512], k == 0, k == 7,
                       W.rd(C_GV, C_GV + 512) + [xT_b], [p6_b])
                act(P, vtok[bi][:], p6[:], AF.Copy, [p6_b], [vtok_b[bi]])
                p7, p7_b = bank()
                for k in range(8):
                    mm(P, p7[:], xT[:, k, cols], w[:, k, C_GG:C_GG + 512], k == 0, k == 7,
                       W.rd(C_GG, C_GG + 512) + [xT_b], [p7_b])
                act(P, gsil[bi][:], p7[:], AF.Silu, [p7_b], [gsil_b[bi]])
                tt(P, "pool", gsil[bi][:], gsil[bi][:], gn[:], ALU.mult, [gsil_b[bi], misc_b],
                   [gsil_b[bi]])
                P.dma("sp", G_d[j * 128:(j + 1) * 128, :], gsil[bi][:], reads=[gsil_b[bi]],
                      writes=[out_b], sem=gsil_b[bi], par=True)
                if HYB_DBG and HYB_DBG <= 5:
                    continue
                p8, p8_b = bank()
                for hp in range(2):
                    mm(P, p8[:, hp * 256:(hp + 1) * 256], keT[bi][:, hp, :],
                       qbd[bi][:, hp, :, :].rearrange("p a t -> p (a t)"), True, True,
                       [keT_b[bi], qbd_b[bi]], [p8_b])
                if HYB_DBG == 7:
                    continue
                tt(P, "dve", atb[bi][:], p8[:].rearrange("p (h t) -> p h t", h=4), C.triu01x4,
                   ALU.mult, [p8_b, C.b], [atb_b[bi]])
                if HYB_DBG == 8:
                    continue
                p9, p9_b = bank()
                for h in range(4):
                    mm(P, p9[:, h * 128:(h + 1) * 128], atb[bi][:, h, :],
                       vtok[bi][:, h * 128:(h + 1) * 128], True, True,
                       [atb_b[bi], vtok_b[bi]], [p9_b])
                if HYB_DBG == 9:
                    continue
                act(P, oisb[bi][:], p9[:], AF.Copy, [p9_b], [oisb_b[bi]])
                P.dma("sp", oi_d[j * 128:(j + 1) * 128, :], oisb[bi][:], reads=[oisb_b[bi]],
                      writes=[out_b], sem=oisb_b[bi], par=True)
                if HYB_DBG and HYB_DBG <= 6:
                    continue
                p10, p10_b = bank()
                for h in range(4):
                    hp = h // 2
                    mm(P, p10[:, h * 128:(h + 1) * 128], ketok[bi][:, hp * 128:(hp + 1) * 128],
                       vtok[bi][:, h * 128:(h + 1) * 128], True, True,
                       [ketok_b[bi], vtok_b[bi]], [p10_b])
                for h in range(4):
                    hp, hh = divmod(h, 2)
                    rows = slice(hh * 64, (hh + 1) * 64)
                    ts(P, "dve", u2[bi][rows, hp, :], p10[rows, h * 128:(h + 1) * 128],
                       eqT[bi][rows, hp, 127:128], None, ALU.mult, None,
                       [p10_b, eqT_b[bi]], [u2_b[bi]], par=(h > 0))
                P.dma("sp", U_d[j, :, :], u2[bi][:].rearrange("p a d -> p (a d)"),
                      reads=[u2_b[bi]], writes=[out_b], sem=u2_b[bi], par=True)
            P.dma("sp", qeT_d[:, :, T * 512:(T + 1) * 512].rearrange("a d t -> d a t"),
                  qest[qeb][:], reads=[qest_b[qeb]], writes=[out_b], sem=qest_b[qeb], par=True)
        P.dma("sp", dd_d, dall[:], reads=[dall_b], writes=[out_b], sem=dall_b, par=True)


def diff_attn_stage(cx, C, TL, li_d, dqT_d, dkTg_d, dvg_d, in_b, lam_d, subg_d, fT_d, out_b,
                    also_reads=None):
    P = cx.P
    NJ = TL // 128
    NG = NJ // 4
    with ExitStack() as st:
        P.barrier()
        KT = [cx.sb([128, 4, TL], BF16, "KT", st) for _ in range(2)]
        V = [cx.sb([128, 4, NJ, 130], BF16, "V", st) for _ in range(2)]
        QT = [[cx.sb([128, TL], BF16, "QT", st) for _ in range(2)] for _ in range(2)]
        kv_b = [Buf("kv") for _ in range(2)]
        for i in range(2):
            P.op("pool", lambda e, t=QT[i][0]: e.memset(t[64:128, :], 0.0), [], [kv_b[i]])
            P.op("pool", lambda e, t=QT[i][1]: e.memset(t[0:64, :], 0.0), [], [kv_b[i]], par=True)
        YT = [cx.sb([128, TL], BF16, "YT", st) for _ in range(2)]
        YT_b = [Buf("YT") for _ in range(2)]
        pt = [[cx.sb([128, 512], BF16, "pt", st) for _ in range(2)] for _ in range(2)]
        pt_b = [[Buf("pt") for _ in range(2)] for _ in range(2)]
        lp = cx.sb([128, 256], F32, "lp", st)
        lt = cx.sb([128, 128], F32, "lt", st)
        lsm = cx.sb([128, 8], F32, "lsm", st)
        gsub = cx.sb([128, 128], F32, "gsub", st)
        lam_b = Buf("lam")
        a_t = [cx.sb([128, 128], F32, "a", st) for _ in range(2)]; a_b = [Buf("a") for _ in range(2)]
        y_t = [cx.sb([128, 128], BF16, "y", st) for _ in range(2)]; y_b = [Buf("y") for _ in range(2)]
        junk = cx.sb([128, 128], BF16, "junk", st)
        sm = [cx.sb([128, 8], F32, "sm", st) for _ in range(2)]; sm_b = [Buf("sm") for _ in range(2)]
        Z = [[cx.ps([128, 512], F32, "Z", st) for _ in range(2)] for _ in range(2)]
        Z_b = [[Buf("Z") for _ in range(2)] for _ in range(2)]
        ACC = [cx.ps([128, 3, 130], F32, "ACC", st) for _ in range(3)]
        ACC_b = [Buf("ACC") for _ in range(3)]
        tpy = cx.ps([128, 128], BF16, "tpy", st); tpy_b = Buf("tpy")

        P.dma("sp", lp[:], lam_d, writes=[lam_b], sem=lam_b)
        P.dma("sp", gsub[:], subg_d, writes=[lam_b], sem=lam_b, par=True)
        tt(P, "dve", lt[:, 0:64], lp[:, 0:64], lp[:, 64:128], ALU.mult, [lam_b], [lam_b])
        tt(P, "dve", lt[:, 64:128], lp[:, 128:192], lp[:, 192:256], ALU.mult, [lam_b], [lam_b])
        P.op("dve", lambda e: e.reduce_sum(lsm[:, 0:1], lt[:, 0:64], AX.X), [lam_b], [lam_b])
        P.op("dve", lambda e: e.reduce_sum(lsm[:, 1:2], lt[:, 64:128], AX.X), [lam_b], [lam_b])
        act(P, lsm[:, 2:4], lsm[:, 0:2], AF.Exp, [lam_b], [lam_b])
        tt(P, "dve", lsm[:, 4:5], lsm[:, 3:4], lsm[:, 2:3], ALU.subtract, [lam_b], [lam_b])
        li = cx.sb([128, 2], F32, "li", st)
        P.dma("sp", li[:], li_d, writes=[lam_b], sem=lam_b, par=True)
        tt(P, "dve", lsm[:, 5:6], lsm[:, 4:5], li[:, 0:1], ALU.add, [lam_b], [lam_b])
        ts(P, "dve", gsub[:], gsub[:], li[:, 1:2], None, ALU.mult, None, [lam_b], [lam_b])
        nlam = lsm[:, 5:6]

        def load_head(h):
            i = h % 2
            P.dma("sp", KT[i][:], dkTg_d[h].rearrange("r d t -> d r t"),
                  reads=_rd(_ib(in_b, h), also_reads), writes=[kv_b[i]], sem=kv_b[i])
            for half in range(2):
                P.dma("sp", V[i][half * 64:(half + 1) * 64], dvg_d[h, half].rearrange(
                    "r p j d -> p r j d"), reads=_rd(_ib(in_b, h), also_reads), writes=[kv_b[i]],
                    sem=kv_b[i], par=True)
            P.dma("sp", QT[i][0][0:64, :], dqT_d[h, 0:64, :], reads=_rd(_ib(in_b, h), also_reads), writes=[kv_b[i]],
                  sem=kv_b[i], par=True)
            P.dma("sp", QT[i][1][64:128, :], dqT_d[h, 64:128, :], reads=_rd(_ib(in_b, h), also_reads), writes=[kv_b[i]],
                  sem=kv_b[i], par=True)

        def acc(m, qb):
            i = m * 4 + qb
            return ACC[i // 3][:, i % 3, :], ACC_b[i // 3]

        load_head(0)
        uidx = 0
        fin = 0
        for h in range(4):
            hi = h % 2
            if h + 1 < 4:
                load_head(h + 1)
            kt, v, qt, kvb = KT[hi], V[hi], QT[hi], kv_b[hi]
            for G in range(NG):
                q0 = G * 512
                units = [(rp, jp) for jp in range(4 * G + 3, -1, -1) for rp in (3, 2, 1, 0)]
                for i in range(3):
                    P.op("dve", lambda e, t=ACC[i]: e.memset(t[:], 0.0), [], [ACC_b[i]])

                def stage_z(u, ui):
                    zi = ui % 2
                    rp, jp = u
                    diag = jp >= 4 * G
                    c0 = max(0, jp - 4 * G) * 128
                    ranges = [(c0, c0 + 128, True), (c0 + 128, 512, False)] if diag else [(0, 512, False)]
                    for m in range(2):
                        for (a, b, isd) in ranges:
                            if a >= b:
                                continue
                            mm(P, Z[m][zi][:, a:b], kt[:, rp, jp * 128:(jp + 1) * 128],
                               qt[m][:, q0 + a:q0 + b], True, not isd, [kvb], [Z_b[m][zi]])
                            if isd:
                                mm(P, Z[m][zi][:, a:b], C.ident, C.msk[:, 1, rp, :], False, True,
                                   [C.b], [Z_b[m][zi]])
                        act(P, pt[m][zi][:, c0:], Z[m][zi][:, c0:], AF.Exp, [Z_b[m][zi]],
                            [pt_b[m][zi]])

                def stage_o(u, ui):
                    zi = ui % 2
                    rp, jp = u
                    c0 = max(0, jp - 4 * G) * 128
                    for m in range(2):
                        for qb in range(c0 // 128, 4):
                            a_ap, a_bf = acc(m, qb)
                            mma(P, a_ap, pt[m][zi][:, qb * 128:(qb + 1) * 128], v[:, rp, jp, :],
                                [pt_b[m][zi], kvb], [a_bf])

                nu = len(units)
                stage_z(units[0], uidx)
                for i in range(nu):
                    if i + 1 < nu:
                        stage_z(units[i + 1], uidx + i + 1)
                    stage_o(units[i], uidx + i)
                uidx += nu
                for qb in range(4):
                    fi = fin % 2; fin += 1
                    s_ = sm[fi]; s_b = sm_b[fi]
                    a0, a0_b = acc(0, qb)
                    a1, a1_b = acc(1, qb)
                    P.op("dve", lambda e, s_=s_, a0=a0: e.reciprocal(s_[:, 0:1], a0[:, 128:129]),
                         [a0_b], [s_b])
                    P.op("dve", lambda e, s_=s_, a1=a1: e.reciprocal(s_[:, 1:2], a1[:, 128:129]),
                         [a1_b], [s_b])
                    tt(P, "dve", s_[:, 2:3], s_[:, 1:2], nlam, ALU.mult, [s_b, lam_b], [s_b])
                    ts(P, "dve", a_t[fi][:], a0[:, 0:128], s_[:, 0:1], None, ALU.mult, None,
                       [a0_b, s_b], [a_b[fi]])
                    stt(P, a_t[fi][:], a1[:, 0:128], s_[:, 2:3], a_t[fi][:], ALU.mult, ALU.add,
                        [a1_b, s_b, a_b[fi]], [a_b[fi]])
                    act(P, junk[:], a_t[fi][:], AF.Square, [a_b[fi]], [s_b], accum_out=s_[:, 3:4])
                    rstd_from_ss(P, s_[:, 4:5], s_[:, 3:4], 128, [s_b], [s_b])
                    stt(P, y_t[fi][:], a_t[fi][:], s_[:, 4:5], gsub[:], ALU.mult, ALU.mult,
                        [a_b[fi], s_b, lam_b], [y_b[fi]])
                    P.op("pe", lambda e, fi=fi: e.transpose(tpy[:], y_t[fi][:], C.ident),
                         [y_b[fi], C.b], [tpy_b])
                    c = q0 + qb * 128
                    tcopy(P, "dve", YT[hi][:, c:c + 128], tpy[:], [tpy_b], [YT_b[hi]],
                          par=not (G == 0 and qb == 0))
            P.dma("sp", fT_d[h, :, :], YT[hi][:], reads=[YT_b[hi]], writes=[out_b],
                  sem=YT_b[hi], par=True)


def gla_scan_stage(cx, C, TL, Ug_d, ddg_d, in_b, qeT_d, oi_d, G_d, onehot_d, fT_d, out_b,
                   also_reads=None):
    P = cx.P
    NJ = TL // 128
    with ExitStack() as st:
        P.barrier()
        dd = cx.sb([128, 4, NJ, 2], F32, "dd", st)
        oh = cx.sb([128, 4], F32, "oh", st)
        qe = cx.sb([128, 2, TL], BF16, "qe", st)
        c_b = Buf("gc")
        P.dma("sp", dd[:], ddg_d.rearrange("r p j a -> p r j a"), reads=_rd(_ib(in_b, "dd"), also_reads), writes=[c_b], sem=c_b)
        P.dma("sp", oh[:], onehot_d, writes=[c_b], sem=c_b, par=True)
        P.dma("sp", qe[:], qeT_d.rearrange("a d t -> d a t"), reads=_rd(_ib(in_b, "dd"), also_reads), writes=[c_b],
              sem=c_b, par=True)
        S2 = cx.sb([128, 2, 128], F32, "S2", st); S2_b = Buf("S2")
        YT = cx.sb([128, 4, TL], BF16, "YT", st); YT_b = Buf("YT")
        def two(shape, dt, nm):
            return [cx.sb(shape, dt, nm, st) for _ in range(2)], [Buf(nm) for _ in range(2)]
        Uj, Uj_b = two([128, 4, 256], F32, "Uj")
        so, so_b = two([128, 2, 128], F32, "sown")
        sob, sob_b = two([128, 2, 2, 128], BF16, "sownb")
        oi, oi_b = two([128, 512], F32, "oi")
        Gt, Gt_b = two([128, 512], F32, "Gt")
        ot, ot_b = two([128, 512], F32, "ot")
        yt, yt_b = two([128, 512], BF16, "yt")
        sm, sm_b = two([128, 8], F32, "sm")
        junk = cx.sb([128, 128], BF16, "junk", st)
        po = [cx.ps([128, 512], F32, "po", st) for _ in range(2)]; po_b = [Buf() for _ in range(2)]
        tpy = [cx.ps([128, 4, 128], BF16, "tpy", st) for _ in range(2)]; tpy_b = [Buf() for _ in range(2)]
        P.op("dve", lambda e: e.memset(S2[:], 0.0), [], [S2_b])
        for i in range(2):
            P.op("pool", lambda e, t=sob[i]: e.memset(t[:], 0.0), [], [sob_b[i]])
        for j in range(NJ):
            bi = j % 2
            JC = Ug_d.shape[2]
            P.dma("sp", Uj[bi][:], Ug_d[j // JC, :, j % JC, :, :].rearrange("r p c -> p r c"),
                  reads=_rd(_ib(in_b, ("u", j // JC)), also_reads), writes=[Uj_b[bi]], sem=Uj_b[bi])
            P.dma("sp", oi[bi][:], oi_d[j * 128:(j + 1) * 128, :], reads=_rd(_ib(in_b, "dd"), also_reads), writes=[oi_b[bi]],
                  sem=oi_b[bi])
            P.dma("sp", Gt[bi][:], G_d[j * 128:(j + 1) * 128, :], reads=_rd(_ib(in_b, "dd"), also_reads), writes=[Gt_b[bi]],
                  sem=Gt_b[bi])
            for rp in range(4):
                if rp == 0:
                    ts(P, "dve", so[bi][:], S2[:], oh[:, 0:1], None, ALU.mult, None,
                       [S2_b, c_b], [so_b[bi]])
                else:
                    stt(P, so[bi][:], S2[:], oh[:, rp:rp + 1], so[bi][:], ALU.mult, ALU.add,
                        [S2_b, c_b, so_b[bi]], [so_b[bi]])
                for hp in range(2):
                    stt(P, S2[:, hp, :], S2[:, hp, :], dd[:, rp, j, hp:hp + 1],
                        Uj[bi][:, rp, hp * 128:(hp + 1) * 128], ALU.mult, ALU.add,
                        [S2_b, c_b, Uj_b[bi]], [S2_b])
            for hh in range(2):
                rows = slice(hh * 64, (hh + 1) * 64)
                tcopy(P, "pool", sob[bi][rows, :, hh, :], so[bi][rows, :, :], [so_b[bi]],
                      [sob_b[bi]], par=(hh > 0))
            for hp in range(2):
                mm(P, po[bi][:, hp * 256:(hp + 1) * 256], qe[:, hp, j * 128:(j + 1) * 128],
                   sob[bi][:, hp, :, :].rearrange("p a d -> p (a d)"), True, True,
                   [c_b, sob_b[bi]], [po_b[bi]])
            tt(P, "dve", ot[bi][:], po[bi][:], oi[bi][:], ALU.add, [po_b[bi], oi_b[bi]], [ot_b[bi]])
            for h in range(4):
                act(P, junk[:], ot[bi][:, h * 128:(h + 1) * 128], AF.Square, [ot_b[bi]],
                    [sm_b[bi]], accum_out=sm[bi][:, h:h + 1])
            rstd_from_ss(P, sm[bi][:, 4:8], sm[bi][:, 0:4], 128, [sm_b[bi]], [sm_b[bi]])
            for h in range(4):
                cs = slice(h * 128, (h + 1) * 128)
                stt(P, yt[bi][:, cs], ot[bi][:, cs], sm[bi][:, 4 + h:5 + h], Gt[bi][:, cs],
                    ALU.mult, ALU.mult, [ot_b[bi], sm_b[bi], Gt_b[bi]], [yt_b[bi]], par=(h > 0))
            for h in range(4):
                P.op("pe", lambda e, bi=bi, h=h: e.transpose(
                    tpy[bi][:, h, :], yt[bi][:, h * 128:(h + 1) * 128], C.ident),
                    [yt_b[bi], C.b], [tpy_b[bi]])
            tcopy(P, "dve", YT[:, :, j * 128:(j + 1) * 128], tpy[bi][:], [tpy_b[bi]], [YT_b],
                  par=(j > 0))
        P.dma("sp", fT_d[4:8, :, :].rearrange("c f t -> f c t"), YT[:], reads=[YT_b],
              writes=[out_b], sem=YT_b, par=True)


def _ffn_inputs(cx, tag):
    return dict(
        wg=cx.dram(tag + "_wg", [128, 8, DFF], F32, "ExternalInput"),
        wu=cx.dram(tag + "_wu", [128, 8, DFF], F32, "ExternalInput"),
        wd=cx.dram(tag + "_wd", [128, NFC, D], F32, "ExternalInput"),
        pre=cx.dram(tag + "_pre", [128, 8], F32, "ExternalInput"),
        post=cx.dram(tag + "_post", [128, D], F32, "ExternalInput"))


def build_A(kind, TL):
    NJ = TL // 128
    cx = Ctx()
    h_in = cx.dram("h_in", [TL, D], F32, "ExternalInput")
    h1 = cx.dram("h1", [TL, D], F32, "ExternalOutput")
    cst = cx.dram("cst", [128, 8, 128], BF16, "ExternalInput")
    msk = cx.dram("msk", [128, 2, 4, 128], BF16, "ExternalInput")
    f = _ffn_inputs(cx, "f0")
    mpre = cx.dram("mpre", [128, 8], F32, "ExternalInput")
    C = load_consts(cx, cst, msk)
    hb0 = [Buf() for _ in range(NJ)]
    hb1 = [Buf() for _ in range(NJ)]
    ob = Buf()
    if kind == "sb":
        wqkv = cx.dram("wqkv", [128, 8, 3 * D], F32, "ExternalInput")
        qT = cx.dram("qT", [8, 128, TL], BF16, "ExternalOutput")
        kT = cx.dram("kT", [8, 128, TL], BF16, "ExternalOutput")
        v = cx.dram("v", [8, 128, NJ, 128], BF16, "ExternalOutput")
        ffn_stage(cx, C, h_in, hb0, h1, hb1, TL, f["wg"], f["wu"], f["wd"], f["pre"], f["post"])
        sb_proj_stage(cx, C, h1, hb1, TL, wqkv, mpre, qT, kT, v, ob)
    else:
        win = cx.dram("win", [128, 8, HYB_IN], F32, "ExternalInput")
        cx.cstf_d = cx.dram("cstf", [128, 2, 128], F32, "ExternalInput")
        wa2 = cx.dram("wa2", [16, 256], F32, "ExternalInput")
        ba = cx.dram("ba", [1, 256], F32, "ExternalInput")
        gnorm = cx.dram("gnorm", [128, 512], F32, "ExternalInput")
        dqT = cx.dram("dqT", [4, 128, TL], BF16, "ExternalOutput")
        dkT = cx.dram("dkT", [4, 128, TL], BF16, "ExternalOutput")
        dv = cx.dram("dv", [4, 128, NJ, 130], BF16, "ExternalOutput")
        qeT = cx.dram("qeT", [2, 128, TL], BF16, "ExternalOutput")
        oi = cx.dram("oi", [TL, 512], F32, "ExternalOutput")
        G = cx.dram("G", [TL, 512], F32, "ExternalOutput")
        U = cx.dram("U", [NJ, 128, 256], F32, "ExternalOutput")
        dd = cx.dram("dd", [128, NJ, 2], F32, "ExternalOutput")
        ffn_stage(cx, C, h_in, hb0, h1, hb1, TL, f["wg"], f["wu"], f["wd"], f["pre"], f["post"])
        hyb_proj_stage(cx, C, h1, hb1, TL, win, mpre, wa2, ba, gnorm, dqT, dkT, dv, qeT, oi, G, U,
                       dd, ob)
    cx.P.finalize(cx.stack)
    return cx


def build_B(kind, TL):
    NJ = TL // 128
    cx = Ctx()
    h1 = cx.dram("h1", [TL, D], F32, "ExternalInput")
    h2 = cx.dram("h2", [TL, D], F32, "Internal")
    h3 = cx.dram("h3", [TL, D], F32, "ExternalOutput")
    fT = cx.dram("fT", [8, 128, TL], BF16, "Internal")
    cst = cx.dram("cst", [128, 8, 128], BF16, "ExternalInput")
    msk = cx.dram("msk", [128, 2, 4, 128], BF16, "ExternalInput")
    wout = cx.dram("wout", [128, 8, D], F32, "ExternalInput")
    mpost = cx.dram("mpost", [128, D], F32, "ExternalInput")
    f = _ffn_inputs(cx, "f1")
    C = load_consts(cx, cst, msk)
    hb1 = [Buf() for _ in range(NJ)]
    hb2 = [Buf() for _ in range(NJ)]
    hb3 = [Buf() for _ in range(NJ)]
    fb = Buf()
    ib = Buf()
    if kind == "sb":
        qT = cx.dram("qT", [8, 128, TL], BF16, "ExternalInput")
        kTg = cx.dram("kTg", [8, 4, 128, TL], BF16, "ExternalInput")
        vg = cx.dram("vg", [8, 4, 128, NJ, 128], BF16, "ExternalInput")
        sb_attn_stage(cx, C, TL, qT, kTg, vg, ib, fT, fb)
    else:
        dqT = cx.dram("dqT", [4, 128, TL], BF16, "ExternalInput")
        dkTg = cx.dram("dkTg", [4, 4, 128, TL], BF16, "ExternalInput")
        dvg = cx.dram("dvg", [4, 2, 4, 64, NJ, 130], BF16, "ExternalInput")
        lam = cx.dram("lam", [128, 256], F32, "ExternalInput")
        subg = cx.dram("subg", [128, 128], F32, "ExternalInput")
        li = cx.dram("li", [128, 2], F32, "ExternalInput")
        JC = min(8, NJ)
        Ug = cx.dram("Ug", [NJ // JC, 4, JC, 128, 256], F32, "ExternalInput")
        ddg = cx.dram("ddg", [4, 128, NJ, 2], F32, "ExternalInput")
        qeT = cx.dram("qeT", [2, 128, TL], BF16, "ExternalInput")
        oi = cx.dram("oi", [TL, 512], F32, "ExternalInput")
        G = cx.dram("G", [TL, 512], F32, "ExternalInput")
        onehot = cx.dram("onehot", [128, 4], F32, "ExternalInput")
        diff_attn_stage(cx, C, TL, li, dqT, dkTg, dvg, ib, lam, subg, fT, fb)
        gla_scan_stage(cx, C, TL, Ug, ddg, ib, qeT, oi, G, onehot, fT, fb)
    outproj_stage(cx, C, TL, fT, fb, h1, hb1, h2, hb2, wout, mpost)
    ffn_stage(cx, C, h2, hb2, h3, hb3, TL, f["wg"], f["wu"], f["wd"], f["pre"], f["post"])
    cx.P.finalize(cx.stack)
    return cx


GROUPS = [[0, 1, 2, 3], [4, 5, 6, 7]]


def build_fused(TL, depth):
    NJ = TL // 128
    cx = Ctx()
    P = cx.P
    ein = lambda n, s, d=F32: cx.dram(n, s, d, "ExternalInput")
    itn = lambda n, s, d=F32: cx.dram(n, s, d, "Internal")
    x_in = ein("x", [TL, D])
    out = cx.dram("out", [TL, D], F32, "ExternalOutput")
    cst = ein("cst", [128, 8, 128], BF16)
    msk = ein("msk", [128, 2, 4, 128], BF16)
    cx.cstf_d = ein("cstf", [128, 2, 128])
    onehot = ein("onehot", [128, 4])
    C = load_consts(cx, cst, msk)
    hs = [itn("hs%d" % i, [TL, D]) for i in range(3)]
    hs_b = [[Buf() for _ in range(NJ)] for _ in range(3)]
    fT = itn("fT", [8, 128, TL], BF16); fT_b = Buf("fT")
    qT = itn("qT", [8, 128, TL], BF16)
    kT = itn("kT", [8, 128, TL], BF16)
    v = itn("v", [8, 128, NJ, 128], BF16)
    kTg = itn("kTg", [8, 4, 128, TL], BF16)
    vg = itn("vg", [8, 4, 128, NJ, 128], BF16)
    dqT = itn("dqT", [4, 128, TL], BF16)
    dkT = itn("dkT", [4, 128, TL], BF16)
    dv = itn("dv", [4, 128, NJ, 130], BF16)
    qeT = itn("qeT", [2, 128, TL], BF16)
    oi = itn("oi", [TL, 512])
    G = itn("G", [TL, 512])
    U = itn("U", [NJ, 128, 256])
    dd = itn("dd", [128, NJ, 2])
    dkTg = itn("dkTg", [4, 4, 128, TL], BF16)
    dvg = itn("dvg", [4, 2, 4, 64, NJ, 130], BF16)
    JC = min(8, NJ)
    Ug = itn("Ug", [NJ // JC, 4, JC, 128, 256])
    ddg = itn("ddg", [4, 128, NJ, 2])
    loc_b = Buf("local")
    gat_sb = {h: Buf("gath_sb%d" % h) for h in range(8)}
    gat_hyb = {h: Buf("gath_hyb%d" % h) for h in range(4)}
    gat_hyb["dd"] = Buf("gath_dd")
    for jc in range(NJ // min(8, NJ)):
        gat_hyb[("u", jc)] = Buf("gath_u%d" % jc)

    def flat2(ap, pat, **kw):
        return ap.rearrange(pat, **kw)

    def gather(src, dst, spat, dpat, gb):
        P.cc_allgather(dst.rearrange(dpat), src.rearrange(spat), GROUPS, reads=[loc_b],
                       writes=[gb], sem=gb)

    cur, cur_b = x_in, [Buf() for _ in range(NJ)]
    for L in range(depth):
        kind = "hyb" if L % 2 == 0 else "sb"
        e = L // 2
        f0 = _ffn_inputs(cx, "L%d_f0" % L)
        f1 = _ffn_inputs(cx, "L%d_f1" % L)
        mpre = ein("L%d_mpre" % L, [128, 8])
        mpost = ein("L%d_mpost" % L, [128, D])
        wout = ein("L%d_wout" % L, [128, 8, D])
        h1, h1_b = hs[0], hs_b[0]
        h2, h2_b = hs[1], hs_b[1]
        if L == depth - 1:
            h3, h3_b = out, [Buf() for _ in range(NJ)]
        else:
            h3, h3_b = hs[2], hs_b[2]
        ffn_stage(cx, C, cur, cur_b, h1, h1_b, TL, f0["wg"], f0["wu"], f0["wd"], f0["pre"], f0["post"])
        if kind == "sb":
            wqkv = ein("L%d_wqkv" % L, [128, 8, 3 * D])
            sb_proj_stage(cx, C, h1, h1_b, TL, wqkv, mpre, qT, kT, v, loc_b)
            for hh in range(8):
                gather(kT[hh], kTg[hh], "d t -> d t", "r d t -> (r d) t", gat_sb[hh])
                gather(v[hh], vg[hh], "p j d -> p (j d)", "r p j d -> (r p) (j d)", gat_sb[hh])
            sb_attn_stage(cx, C, TL, qT, kTg, vg, gat_sb, fT, fT_b, also_reads=loc_b)
        else:
            win = ein("L%d_win" % L, [128, 8, HYB_IN])
            wa2 = ein("L%d_wa2" % L, [16, 256])
            ba = ein("L%d_ba" % L, [1, 256])
            gnorm = ein("L%d_gnorm" % L, [128, 512])
            lam = ein("L%d_lam" % L, [128, 256])
            subg = ein("L%d_subg" % L, [128, 128])
            li = ein("L%d_li" % L, [128, 2])
            hyb_proj_stage(cx, C, h1, h1_b, TL, win, mpre, wa2, ba, gnorm, dqT, dkT, dv, qeT, oi,
                           G, U, dd, loc_b)
            for hh in range(4):
                gather(dkT[hh], dkTg[hh], "d t -> d t", "r d t -> (r d) t", gat_hyb[hh])
                for half in range(2):
                    gather(dv[hh, half * 64:(half + 1) * 64], dvg[hh, half], "p j d -> p (j d)",
                           "r p j d -> (r p) (j d)", gat_hyb[hh])
            gather(dd, ddg, "p j a -> p (j a)", "r p j a -> (r p) (j a)", gat_hyb["dd"])
            for jc in range(NJ // JC):
                gather(U[jc * JC:(jc + 1) * JC], Ug[jc], "j p c -> (j p) c", "r j p c -> (r j p) c",
                       gat_hyb[("u", jc)])
            diff_attn_stage(cx, C, TL, li, dqT, dkTg, dvg, gat_hyb, lam, subg, fT, fT_b,
                            also_reads=loc_b)
            gla_scan_stage(cx, C, TL, Ug, ddg, gat_hyb, qeT, oi, G, onehot, fT, fT_b,
                           also_reads=loc_b)
        outproj_stage(cx, C, TL, fT, fT_b, h1, h1_b, h2, h2_b, wout, mpost)
        ffn_stage(cx, C, h2, h2_b, h3, h3_b, TL, f1["wg"], f1["wu"], f1["wd"], f1["pre"], f1["post"])
        cur, cur_b = h3, h3_b
    cx.P.finalize(cx.stack)
    return cx


def _wl(W):
    kc = W.shape[0] // 128
    return np.ascontiguousarray(W.reshape(kc, 128, W.shape[1]).transpose(1, 0, 2))


def _colT(g):
    return np.ascontiguousarray(g.reshape(8, 128).T)


def _rep(v, n=128):
    return np.ascontiguousarray(np.broadcast_to(v, (n,) + v.shape))


def _ffn_maps(tag, wg, wu, wd, pre, post):
    return {tag + "_wg": _wl(wg), tag + "_wu": _wl(wu), tag + "_wd": _wl(wd),
            tag + "_pre": _colT(pre), tag + "_post": _rep(post)}


def _shard_tokens(x, NJ):
    out = []
    for c in range(8):
        b, r = divmod(c, 4)
        out.append(np.ascontiguousarray(
            x[b].reshape(NJ, 4, 128, x.shape[-1])[:, r].reshape(NJ * 128, x.shape[-1])))
    return out


def _unshard_tokens(lst, S, NJ):
    out = np.zeros((2, S, lst[0].shape[-1]), lst[0].dtype)
    for c in range(8):
        b, r = divmod(c, 4)
        out[b].reshape(NJ, 4, 128, lst[0].shape[-1])[:, r] = lst[c].reshape(NJ, 128, -1)
    return out


_PROGS = {}


def _prog(which, kind, TL):
    k = (which, kind, TL)
    if k not in _PROGS:
        _PROGS[k] = build_A(kind, TL) if which == "A" else build_B(kind, TL)
    return _PROGS[k]


def _run(cx, in_maps):
    return run_bass_kernel_spmd(cx.nc, in_maps, core_ids=list(range(8))).results


def kernel_unfused(x, ffn_pre_g, ffn_post_g, ffn_w_gate, ffn_w_up, ffn_w_down, mix_pre_g, mix_post_g,
           hyb_w_in, hyb_w_out, diff_lambda, diff_subln_g, gla_w_a2, gla_b_a, gla_norm_g,
           sb_w_qkv, sb_w_out):
    f32 = lambda a: np.asarray(a, dtype=np.float32)
    x = f32(x)
    S = x.shape[1]
    NJ = S // 512
    TL = NJ * 128
    depth = mix_pre_g.shape[0]
    cst = host_consts()
    cstf = host_consts_f32()
    msks = [host_masks(c % 4) for c in range(8)]
    h = _shard_tokens(x, NJ)
    for L in range(depth):
        kind = "hyb" if L % 2 == 0 else "sb"
        e = L // 2
        common = dict(cst=cst)
        a_in = dict(common)
        a_in.update(_ffn_maps("f0", f32(ffn_w_gate[L, 0]), f32(ffn_w_up[L, 0]),
                              f32(ffn_w_down[L, 0]), f32(ffn_pre_g[L, 0]), f32(ffn_post_g[L, 0])))
        a_in["mpre"] = _colT(f32(mix_pre_g[L]))
        if kind == "sb":
            a_in["wqkv"] = _wl(f32(sb_w_qkv[e]))
        else:
            a_in["win"] = _wl(f32(hyb_w_in[e]))
            a_in["cstf"] = cstf
            a_in["wa2"] = np.ascontiguousarray(f32(gla_w_a2[e]))
            a_in["ba"] = np.ascontiguousarray(f32(gla_b_a[e]).reshape(1, 256))
            a_in["gnorm"] = _rep(np.tile(f32(gla_norm_g[e]), 4))
        rA = _run(_prog("A", kind, TL),
                  [dict(a_in, h_in=h[c], msk=msks[c]) for c in range(8)])
        b_in = dict(common)
        b_in.update(_ffn_maps("f1", f32(ffn_w_gate[L, 1]), f32(ffn_w_up[L, 1]),
                              f32(ffn_w_down[L, 1]), f32(ffn_pre_g[L, 1]), f32(ffn_post_g[L, 1])))
        b_in["wout"] = _wl(f32(sb_w_out[e] if kind == "sb" else hyb_w_out[e]))
        b_in["mpost"] = _rep(f32(mix_post_g[L]))
        maps = []
        for c in range(8):
            b, r = divmod(c, 4)
            grp = [rA[4 * b + rp] for rp in range(4)]
            m = dict(b_in, h1=rA[c]["h1"], msk=msks[c])
            if kind == "sb":
                m["qT"] = rA[c]["qT"]
                m["kTg"] = np.stack([g["kT"] for g in grp], axis=1)
                m["vg"] = np.stack([g["v"] for g in grp], axis=1)
            else:
                lambda_init = 0.8 - 0.6 * math.exp(-0.3 * L)
                oh = np.zeros((128, 4), np.float32)
                oh[:, r] = 1.0
                li = np.zeros((128, 2), np.float32)
                li[:, 0] = -lambda_init
                li[:, 1] = 1.0 - lambda_init
                JC = min(8, NJ)
                dvs = np.stack([g["dv"].reshape(4, 2, 64, NJ, 130) for g in grp], axis=2)
                Us = np.stack([g["U"].reshape(NJ // JC, JC, 128, 256) for g in grp], axis=1)
                m.update(dqT=rA[c]["dqT"], dkTg=np.stack([g["dkT"] for g in grp], axis=1),
                         dvg=dvs,
                         lam=_rep(f32(diff_lambda[e]).reshape(256)),
                         subg=_rep(f32(diff_subln_g[e])), li=li,
                         Ug=Us, ddg=np.stack([g["dd"] for g in grp]),
                         qeT=rA[c]["qeT"], oi=rA[c]["oi"], G=rA[c]["G"], onehot=oh)
            maps.append(m)
        rB = _run(_prog("B", kind, TL), maps)
        h = [rB[c]["h3"] for c in range(8)]
    return _unshard_tokens(h, S, NJ).astype(np.float32)


def kernel(x, ffn_pre_g, ffn_post_g, ffn_w_gate, ffn_w_up, ffn_w_down, mix_pre_g, mix_post_g,
           hyb_w_in, hyb_w_out, diff_lambda, diff_subln_g, gla_w_a2, gla_b_a, gla_norm_g,
           sb_w_qkv, sb_w_out):
    f32 = lambda a: np.asarray(a, dtype=np.float32)
    x = f32(x)
    S = x.shape[1]
    NJ = S // 512
    TL = NJ * 128
    depth = mix_pre_g.shape[0]
    key = ("fused", TL, depth)
    if key not in _PROGS:
        _PROGS[key] = build_fused(TL, depth)
    cx = _PROGS[key]
    shared = dict(cst=host_consts(), cstf=host_consts_f32())
    for L in range(depth):
        e = L // 2
        t = "L%d_" % L
        for i in range(2):
            shared.update(_ffn_maps(t + "f%d" % i, f32(ffn_w_gate[L, i]), f32(ffn_w_up[L, i]),
                                    f32(ffn_w_down[L, i]), f32(ffn_pre_g[L, i]),
                                    f32(ffn_post_g[L, i])))
        shared[t + "mpre"] = _colT(f32(mix_pre_g[L]))
        shared[t + "mpost"] = _rep(f32(mix_post_g[L]))
        if L % 2 == 1:
            shared[t + "wqkv"] = _wl(f32(sb_w_qkv[e]))
            shared[t + "wout"] = _wl(f32(sb_w_out[e]))
        else:
            lambda_init = 0.8 - 0.6 * math.exp(-0.3 * L)
            li = np.zeros((128, 2), np.float32)
            li[:, 0] = -lambda_init
            li[:, 1] = 1.0 - lambda_init
            shared[t + "win"] = _wl(f32(hyb_w_in[e]))
            shared[t + "wout"] = _wl(f32(hyb_w_out[e]))
            shared[t + "wa2"] = np.ascontiguousarray(f32(gla_w_a2[e]))
            shared[t + "ba"] = np.ascontiguousarray(f32(gla_b_a[e]).reshape(1, 256))
            shared[t + "gnorm"] = _rep(np.tile(f32(gla_norm_g[e]), 4))
            shared[t + "lam"] = _rep(f32(diff_lambda[e]).reshape(256))
            shared[t + "subg"] = _rep(f32(diff_subln_g[e]))
            shared[t + "li"] = li
    xs = _shard_tokens(x, NJ)
    maps = []
    for c in range(8):
        oh = np.zeros((128, 4), np.float32)
        oh[:, c % 4] = 1.0
        maps.append(dict(shared, x=xs[c], msk=host_masks(c % 4), onehot=oh))
    res = _run(cx, maps)
    return _unshard_tokens([res[c]["out"] for c in range(8)], S, NJ).astype(np.float32)
```

```python
import math
from contextlib import ExitStack

import numpy as np
import ml_dtypes

import concourse.bass as bass
import concourse.mybir as mybir
from concourse.bass_utils import run_bass_kernel_spmd

F32 = mybir.dt.float32
BF16 = mybir.dt.bfloat16
AF = mybir.ActivationFunctionType
ALU = mybir.AluOpType
AX = mybir.AxisListType

D = 1024
DFF = 1408
NFC = DFF // 128
EPS = 1e-6
NEG = -30000.0
HYB_IN = 3088


class Buf:
    __slots__ = ("name", "w", "r", "pw", "pr")

    def __init__(self, name=""):
        self.name = name
        self.w = []
        self.r = []
        self.pw = []
        self.pr = []


class Op:
    __slots__ = ("eng", "fn", "deps", "dma", "sem", "val", "signal", "idx", "dinc")

    def __init__(self, eng, fn, dma):
        self.eng = eng
        self.fn = fn
        self.dma = dma
        self.deps = []
        self.sem = None
        self.val = 0
        self.signal = False
        self.idx = 0
        self.dinc = 16


ENGS = ("pe", "act", "dve", "pool", "sp")
N_DMA_SEMS = 80


class Prog:
    def __init__(self, nc, same_engine_sync=True):
        self.nc = nc
        self.ops = {e: [] for e in ENGS}
        self.all = []
        self.same_engine_sync = same_engine_sync
        self.bar_from = 0

    def op(self, eng, fn, reads=(), writes=(), dma=None, par=False):
        o = Op(eng, fn, dma)
        o.idx = len(self.all)
        self.all.append(o)
        deps = {}
        for b in reads:
            for w in b.w:
                deps[id(w)] = w
        for b in writes:
            if b.r or (not par) or (not b.w):
                b.pw = b.w
                b.pr = b.r
                b.w = []
                b.r = []
            for x in b.pw:
                deps[id(x)] = x
            for x in b.pr:
                deps[id(x)] = x
            b.w.append(o)
        for b in reads:
            if b not in writes:
                b.r.append(o)
        for d in deps.values():
            if d is o:
                continue
            if d.dma is None and d.eng == eng:
                if eng == "pe" or not self.same_engine_sync:
                    continue
            o.deps.append(d)
            d.signal = True
        self.ops[eng].append(o)
        return o

    def cc_allgather(self, out, in_, groups, reads=(), writes=(), sem=None):
        o = self.op("pool", lambda e: e.collective_compute(
            "AllGather", ALU.bypass, replica_groups=groups, ins=[in_], outs=[out]),
            reads, writes, dma=sem, par=True)
        o.dinc = 1
        return o

    def barrier(self):
        lasts = []
        for e in ENGS:
            for o in reversed(self.ops[e]):
                if o.fn is not None and o.dma is None:
                    lasts.append(o)
                    break
        dmas = [o for o in self.all[self.bar_from:] if o.dma is not None and o.dinc == 16]
        self.bar_from = len(self.all)
        for e in ENGS:
            o = Op(e, None, None)
            o.idx = len(self.all)
            self.all.append(o)
            o.deps = [d for d in lasts if d.eng != e] + dmas
            for d in o.deps:
                d.signal = True
            self.ops[e].append(o)

    def dma(self, q, out, in_, reads=(), writes=(), sem=None, par=False, **kw):
        return self.op(q, lambda e: e.dma_start(out=out, in_=in_, **kw), reads, writes,
                       dma=sem, par=par)

    def finalize(self, stack):
        nc = self.nc
        fin = Op("sp", None, None)
        fin.idx = len(self.all)
        fin.deps = [o for o in self.all if o.dma is not None]
        self.all.append(fin)
        self.ops["sp"].append(fin)
        esem = {e: stack.enter_context(nc.semaphore("s_" + e)) for e in ENGS}
        cnt = {e: 0 for e in ENGS}
        last_use = {}
        for o in self.all:
            if o.dma is not None:
                last_use[id(o.dma)] = o.idx
        pool = [stack.enter_context(nc.semaphore("d_%d" % i)) for i in range(N_DMA_SEMS)]
        pcount = [0] * N_DMA_SEMS
        free = list(range(N_DMA_SEMS))
        owner = {}
        release_at = {}
        for o in self.all:
            if o.dma is not None:
                k = id(o.dma)
                if k not in owner:
                    assert free, "out of DMA semaphores"
                    owner[k] = free.pop(0)
                si = owner[k]
                pcount[si] += o.dinc
                o.sem = pool[si]
                o.val = pcount[si]
                if last_use[k] == o.idx:
                    free.append(si)
                    del owner[k]
            elif o.signal:
                cnt[o.eng] += 1
                o.sem = esem[o.eng]
                o.val = cnt[o.eng]
        self.counts = dict(cnt)

        def replay(ename, eng):
            waited = {}
            for o in self.ops[ename]:
                need = {}
                for d in o.deps:
                    k = id(d.sem)
                    if waited.get(k, 0) >= d.val:
                        continue
                    if k not in need or need[k][1] < d.val:
                        need[k] = (d.sem, d.val)
                for k, (s, v) in need.items():
                    eng.wait_ge(s, v)
                    waited[k] = v
                if o.fn is None:
                    continue
                ins = o.fn(eng)
                if o.dma is not None:
                    ins.then_inc(o.sem, o.dinc)
                elif o.signal:
                    ins.then_inc(o.sem, 1)

        with nc.Block() as block:
            @block.sync
            def _(eng):
                replay("sp", eng)

            @block.tensor
            def _(eng):
                replay("pe", eng)

            @block.scalar
            def _(eng):
                replay("act", eng)

            @block.vector
            def _(eng):
                replay("dve", eng)

            @block.gpsimd
            def _(eng):
                replay("pool", eng)


class Ctx:
    def __init__(self, same_engine_sync=True):
        self.nc = bass.Bass("TRN2", target_bir_lowering=False)
        self.P = Prog(self.nc, same_engine_sync)
        self.stack = ExitStack()
        self.names = 0

    def sb(self, shape, dt, name=None, stack=None):
        self.names += 1
        t = (stack or self.stack).enter_context(
            self.nc.sbuf_tensor("%s_%d" % (name or "sb", self.names), list(shape), dt))
        return t

    def ps(self, shape, dt=F32, name=None, stack=None):
        self.names += 1
        t = (stack or self.stack).enter_context(
            self.nc.psum_tensor("%s_%d" % (name or "ps", self.names), list(shape), dt))
        return t

    def dram(self, name, shape, dt, kind="Internal"):
        return self.nc.dram_tensor(name, list(shape), dt, kind=kind).ap()


def mm(P, out, lhsT, rhs, start, stop, reads, writes):
    return P.op("pe", lambda e: e.matmul(out, lhsT, rhs, start=start, stop=stop),
                reads, writes)


def _rd(a, b):
    return [a] if b is None else [a, b]


def _ib(in_b, key):
    return in_b[key] if isinstance(in_b, dict) else in_b


def mma(P, out, lhsT, rhs, reads, writes):
    return P.op("pe", lambda e: e.matmul(out, lhsT, rhs, start=False, stop=True,
                                         skip_group_check=True), reads, writes)


def tcopy(P, eng, out, in_, reads, writes, par=False):
    return P.op(eng, lambda e: e.tensor_copy(out=out, in_=in_), reads, writes, par=par)


def tt(P, eng, out, a, b, op, reads, writes, par=False):
    return P.op(eng, lambda e: e.tensor_tensor(out, a, b, op), reads, writes, par=par)


def ts(P, eng, out, a, s1, s2, op0, op1, reads, writes, par=False):
    if s2 is None:
        return P.op(eng, lambda e: e.tensor_scalar(out, a, s1, None, op0), reads, writes, par=par)
    return P.op(eng, lambda e: e.tensor_scalar(out, a, s1, s2, op0, op1), reads, writes, par=par)


def stt(P, out, a, s, b, op0, op1, reads, writes, par=False):
    return P.op("dve", lambda e: e.scalar_tensor_tensor(out, a, s, b, op0, op1), reads, writes,
                par=par)


def act(P, out, in_, func, reads, writes, **kw):
    return P.op("act", lambda e: e.activation(out=out, in_=in_, func=func, **kw),
                reads, writes)


def load_weight(cx, dst, dstbuf, src, KC, N, gcol=None, mul=None, stg=None, eng_rr=("dve", "pool")):
    P = cx.P
    stiles, sbufs, state = stg
    CH = stiles[0].shape[-1]
    for k in range(KC):
        for c0 in range(0, N, CH):
            cw = min(CH, N - c0)
            i = state[0] % len(stiles)
            state[0] += 1
            st, sbf = stiles[i], sbufs[i]
            P.dma("sp", st[:, 0:cw], src[:, k, c0:c0 + cw], writes=[sbf], sem=sbf)
            en = eng_rr[state[0] % len(eng_rr)]
            o_ap = dst[:, k, c0:c0 + cw]
            i_ap = st[:, 0:cw]
            if gcol is not None:
                s1 = gcol[0][:, k:k + 1]
                rd = [sbf, gcol[1]]
                if mul is not None:
                    P.op(en, lambda e, o=o_ap, i=i_ap, s=s1: e.tensor_scalar(
                        o, i, s, float(mul), ALU.mult, ALU.mult), rd, [dstbuf], par=True)
                else:
                    P.op(en, lambda e, o=o_ap, i=i_ap, s=s1: e.tensor_scalar(
                        o, i, s, None, ALU.mult), rd, [dstbuf], par=True)
            else:
                P.op(en, lambda e, o=o_ap, i=i_ap: e.tensor_copy(out=o, in_=i),
                     [sbf], [dstbuf], par=True)


def make_stage(cx, n=4, ch=1408, stack=None):
    tiles = [cx.sb([128, ch], F32, "wstg", stack) for _ in range(n)]
    bufs = [Buf("wstg%d" % i) for i in range(n)]
    return (tiles, bufs, [0])


def rstd_from_ss(P, rstd_ap, ss_ap, n, reads, writes, extra_mul=None):
    act(P, rstd_ap, ss_ap, AF.Ln, reads, writes, scale=1.0 / n, bias=EPS)
    b = 0.0 if extra_mul is None else math.log(extra_mul)
    act(P, rstd_ap, rstd_ap, AF.Exp, writes, writes, scale=-0.5, bias=b)


def load_weight_chunks(cx, dst, bufs, src, KC, N, CW, gcol=None, mul=None, stg=None,
                       eng_rr=("pool", "dve")):
    P = cx.P
    stiles, sbufs, state = stg
    for ci, c0 in enumerate(range(0, N, CW)):
        cw = min(CW, N - c0)
        i = state[0] % len(stiles)
        state[0] += 1
        st, sbf = stiles[i], sbufs[i]
        P.dma("sp", st[:, 0:KC, 0:cw], src[:, :, c0:c0 + cw], writes=[sbf], sem=sbf)
        for k in range(KC):
            en = eng_rr[(state[0] + k) % len(eng_rr)]
            o_ap = dst[:, k, c0:c0 + cw]
            i_ap = st[:, k, 0:cw]
            if gcol is not None:
                ts(P, en, o_ap, i_ap, gcol[0][:, k:k + 1], (None if mul is None else float(mul)),
                   ALU.mult, ALU.mult, [sbf, gcol[1]], [bufs[ci]], par=(k > 0))
            else:
                tcopy(P, en, o_ap, i_ap, [sbf], [bufs[ci]], par=(k > 0))


def make_stage3(cx, n, KC, CW, stack=None):
    tiles = [cx.sb([128, KC, CW], F32, "wstg3", stack) for _ in range(n)]
    bufs = [Buf("wstg3_%d" % i) for i in range(n)]
    return (tiles, bufs, [0])


class ChunkedW:
    def __init__(self, cx, st, name, KC, N, CW):
        self.cx, self.KC, self.N, self.CW = cx, KC, N, CW
        self.t = cx.sb([128, KC, N], BF16, name, st)
        self.nch = (N + CW - 1) // CW
        self.bufs = [Buf("%s_c%d" % (name, i)) for i in range(self.nch)]
        self.stg = make_stage3(cx, 3, KC, CW, st)

    def load(self, src_d, gcol, mul_of, order=None):
        for ci in (order if order is not None else range(self.nch)):
            c0 = ci * self.CW
            c1 = min(self.N, c0 + self.CW)
            load_weight_chunks(self.cx, self.t[:, :, c0:c1], [self.bufs[ci]], src_d[:, :, c0:c1],
                               self.KC, c1 - c0, self.CW, gcol=gcol, mul=mul_of(c0), stg=self.stg)

    def rd(self, c0, c1):
        return [self.bufs[i] for i in range(c0 // self.CW, (c1 - 1) // self.CW + 1)]


class NormPipe:
    def __init__(self, cx, C, st, h_in, h_in_b, ntp=2):
        self.cx, self.C, self.h_in, self.h_in_b = cx, C, h_in, h_in_b
        self.ntp = ntp
        self.hx = [cx.sb([128, 4, D], F32, "hx", st) for _ in range(2)]
        self.hx_b = [[Buf("hx") for _ in range(4)] for _ in range(2)]
        self.xnT = [cx.sb([128, 8, 512], BF16, "xnT", st) for _ in range(2)]
        self.xnT_b = [Buf("xnT") for _ in range(2)]
        self.xn = [cx.sb([128, D], BF16, "xn", st) for _ in range(4)]
        self.xn_b = [Buf("xn") for _ in range(4)]
        self.sm = [cx.sb([128, 8], F32, "nsm", st) for _ in range(8)]
        self.sm_b = [Buf("nsm") for _ in range(8)]
        self.junk = cx.sb([128, D], BF16, "njunk", st)
        self.tp = [cx.ps([128, 8, 128], BF16, "tp", st) for _ in range(ntp)]
        self.tp_b = [Buf("tp") for _ in range(ntp)]

    def loads(self, T):
        P = self.cx.P
        hb = T % 2
        for sb in range(4):
            ti = T * 4 + sb
            P.dma("sp", self.hx[hb][:, sb, :], self.h_in[ti * 128:(ti + 1) * 128, :],
                  reads=[self.h_in_b[ti]], writes=[self.hx_b[hb][sb]], sem=self.hx_b[hb][sb])

    def act_part(self, T):
        P = self.cx.P
        hb = T % 2
        for sb in range(4):
            s_ = self.sm[hb * 4 + sb]; s_b = self.sm_b[hb * 4 + sb]
            x_ap = self.hx[hb][:, sb, :]
            act(P, self.junk[:], x_ap, AF.Square, [self.hx_b[hb][sb]], [s_b], accum_out=s_[:, 0:1])
            rstd_from_ss(P, s_[:, 1:2], s_[:, 0:1], D, [s_b], [s_b])
            act(P, self.xn[sb][:], x_ap, AF.Copy, [self.hx_b[hb][sb], s_b], [self.xn_b[sb]],
                scale=s_[:, 1:2])

    def pe_part(self, T, sb):
        P = self.cx.P
        hb = T % 2
        i2 = sb % self.ntp
        tp, tp_b = self.tp[i2], self.tp_b[i2]
        xn = self.xn[sb]
        for k in range(8):
            P.op("pe", lambda e, k=k, tp=tp, xn=xn: e.transpose(
                tp[:, k, :], xn[:, k * 128:(k + 1) * 128], self.C.ident),
                [self.xn_b[sb], self.C.ident_b], [tp_b])
        tcopy(P, "dve", self.xnT[hb][:, :, sb * 128:(sb + 1) * 128], tp[:], [tp_b],
              [self.xnT_b[hb]], par=(sb > 0))


class Consts:
    pass


def load_consts(cx, cst_dram, msk_dram):
    P = cx.P
    C = Consts()
    C.cst = cx.sb([128, 8, 128], BF16, "cst")
    C.b = Buf("cst")
    P.dma("sp", C.cst[:], cst_dram, writes=[C.b], sem=C.b)
    C.msk = cx.sb([128, 2, 4, 128], BF16, "msk")
    P.dma("sp", C.msk[:], msk_dram, writes=[C.b], sem=C.b, par=True)
    C.ident = C.cst[:, 0, :]
    C.ntri = C.cst[:, 1, :]
    C.nones = C.cst[:, 2, :]
    C.triu01 = C.cst[:, 3, :]
    C.triu01x4 = C.cst[:, 4:8, :]
    C.ident_b = C.b
    return C


def load_consts_f32(cx, C, cstf_dram, stack=None):
    C.cstf = cx.sb([128, 2, 128], F32, "cstf", stack)
    C.fb = Buf("cstf")
    cx.P.dma("sp", C.cstf[:], cstf_dram, writes=[C.fb], sem=C.fb)
    C.triu01f = C.cstf[:, 0, :]
    C.onesf = C.cstf[:, 1, :]


def host_consts_f32():
    i = np.arange(128)
    c = np.zeros((128, 2, 128), np.float32)
    c[:, 0] = (i[:, None] <= i[None, :])
    c[:, 1] = 1.0
    return c


def host_consts():
    i = np.arange(128)
    cst = np.zeros((128, 8, 128), np.float32)
    cst[:, 0] = np.eye(128)
    cst[:, 1] = -(i[:, None] >= i[None, :]).astype(np.float32)
    cst[:, 2] = -1.0
    cst[:, 3] = (i[:, None] <= i[None, :]).astype(np.float32)
    for k in range(4, 8):
        cst[:, k] = cst[:, 3]
    return cst.astype(ml_dtypes.bfloat16)


def host_masks(r):
    i = np.arange(128)
    m = np.zeros((128, 2, 4, 128), np.float32)
    for rp in range(4):
        if rp > r:
            m[:, :, rp] = NEG
        elif rp == r:
            m[:, 0, rp] = np.where(i[:, None] >= i[None, :], NEG, 0.0)
            m[:, 1, rp] = np.where(i[:, None] > i[None, :], NEG, 0.0)
    return m.astype(ml_dtypes.bfloat16)


def norm_transpose(cx, C, x_ap, xbuf, xnT, xnT_buf, col0, ss, rstd, small_b, junk, junk_b,
                   xn, xn_b, tp, tp_b):
    P = cx.P
    act(P, junk[:], x_ap, AF.Square, [xbuf], [small_b], accum_out=ss)
    rstd_from_ss(P, rstd, ss, D, [small_b], [small_b])
    act(P, xn[:], x_ap, AF.Copy, [xbuf, small_b], [xn_b], scale=rstd)
    for k in range(8):
        P.op("pe", lambda e, k=k: e.transpose(tp[:, k, :], xn[:, k * 128:(k + 1) * 128],
                                              C.ident),
             [xn_b, C.ident_b], [tp_b])
    P.op("dve", lambda e: e.tensor_copy(out=xnT[:, :, col0:col0 + 128], in_=tp[:]),
         [tp_b], [xnT_buf], par=True)


def resid_tail(P, s_, s_b, po, po_b, gpost, gpost_b, tmp, tmp_b, hx_ap, hx_b, h_out_ap,
               h_out_b, mul):
    P.op("dve", lambda e: e.tensor_tensor(s_[:, 2:3], s_[:, 0:1], s_[:, 1:2], ALU.add),
         [s_b], [s_b])
    rstd_from_ss(P, s_[:, 3:4], s_[:, 2:3], D, [s_b], [s_b],
                 extra_mul=(None if mul == 1.0 else mul))
    for half in range(2):
        P.op("dve", lambda e, half=half: e.tensor_tensor(
            tmp[:, half * 512:(half + 1) * 512], po[half][:],
            gpost[:, half * 512:(half + 1) * 512], ALU.mult),
            [po_b[half], gpost_b], [tmp_b], par=(half == 1))
    P.op("dve", lambda e: e.scalar_tensor_tensor(
        tmp[:], tmp[:], s_[:, 3:4], hx_ap, ALU.mult, ALU.add),
        [tmp_b, s_b, hx_b], [tmp_b])
    P.dma("sp", h_out_ap, tmp[:], reads=[tmp_b], writes=[h_out_b], sem=tmp_b)


def ffn_stage(cx, C, h_in, h_in_b, h_out, h_out_b, TL, wg_d, wu_d, wd_d, pre_gT_d, post_g_d):
    P = cx.P
    NT = TL // 512
    with ExitStack() as st:
        P.barrier()
        wg = cx.sb([128, 8, DFF], BF16, "wg", st); wg_b = [Buf("wg") for _ in range(NFC)]
        wu = cx.sb([128, 8, DFF], BF16, "wu", st); wu_b = [Buf("wu") for _ in range(NFC)]
        wd = cx.sb([128, NFC, D], BF16, "wd", st); wd_b = [Buf("wd") for _ in range(4)]
        pg = cx.sb([128, 8], F32, "pre_g", st); pg_b = Buf("pre_g")
        gpost = cx.sb([128, D], F32, "gpost", st); gpost_b = Buf("gpost")
        stg = make_stage3(cx, 3, NFC, 256, st)
        np_ = NormPipe(cx, C, st, h_in, h_in_b, ntp=1)
        np_.loads(0)
        P.dma("sp", pg[:], pre_gT_d, writes=[pg_b], sem=pg_b)
        P.dma("sp", gpost[:], post_g_d, writes=[gpost_b], sem=gpost_b)
        for fc in range(NFC):
            c0 = fc * 128
            for (w_, wb_, src_) in ((wg, wg_b, wg_d), (wu, wu_b, wu_d)):
                load_weight_chunks(cx, w_[:, :, c0:c0 + 128], [wb_[fc]], src_[:, :, c0:c0 + 128],
                                   8, 128, 128, gcol=(pg, pg_b), stg=stg)
        for q in range(4):
            load_weight_chunks(cx, wd[:, :, q * 256:(q + 1) * 256], [wd_b[q]],
                               wd_d[:, :, q * 256:(q + 1) * 256], NFC, 256, 256, stg=stg)

        fT = cx.sb([128, NFC, 512], BF16, "fT", st); fT_b = [Buf("fT") for _ in range(NFC)]
        junk = cx.sb([128, 512], BF16, "junk", st)
        small = [cx.sb([128, 8], F32, "small", st) for _ in range(4)]
        small_b = [Buf("small") for _ in range(4)]
        sg = [cx.sb([128, 512], F32, "sg", st) for _ in range(2)]; sg_b = [Buf("sg") for _ in range(2)]
        tmp = [cx.sb([128, D], F32, "tmp", st) for _ in range(2)]; tmp_b = [Buf("tmp") for _ in range(2)]
        pgate = [cx.ps([128, 512], F32, "pgate", st) for _ in range(2)]; pgate_b = [Buf() for _ in range(2)]
        pup = [cx.ps([128, 512], F32, "pup", st) for _ in range(2)]; pup_b = [Buf() for _ in range(2)]
        po3 = [cx.ps([128, 512], F32, "po", st) for _ in range(3)]; po3_b = [Buf() for _ in range(3)]
        pcount = 0

        np_.act_part(0)
        for sb in range(4):
            np_.pe_part(0, sb)
        sidx = 0
        for T in range(NT):
            hb = T % 2
            xT, xT_b = np_.xnT[hb], np_.xnT_b[hb]
            if T + 1 < NT:
                np_.loads(T + 1)
                np_.act_part(T + 1)
            for fc in range(NFC):
                i2 = fc % 2
                for k in range(8):
                    mm(P, pgate[i2][:], wg[:, k, fc * 128:(fc + 1) * 128], xT[:, k, :],
                       k == 0, k == 7, [wg_b[fc], xT_b], [pgate_b[i2]])
                for k in range(8):
                    mm(P, pup[i2][:], wu[:, k, fc * 128:(fc + 1) * 128], xT[:, k, :],
                       k == 0, k == 7, [wu_b[fc], xT_b], [pup_b[i2]])
                act(P, sg[i2][:], pgate[i2][:], AF.Silu, [pgate_b[i2]], [sg_b[i2]])
                tt(P, "dve", fT[:, fc, :], sg[i2][:], pup[i2][:], ALU.mult,
                   [sg_b[i2], pup_b[i2]], [fT_b[fc]])
                if T + 1 < NT and fc in (2, 4, 6, 8):
                    np_.pe_part(T + 1, (fc - 2) // 2)
            for sb in range(4):
                ti = T * 4 + sb
                s_ = small[sidx % 4]; s_b = small_b[sidx % 4]; sidx += 1
                i2 = sb % 2
                bk = [pcount % 3, (pcount + 1) % 3]; pcount += 2
                po = [po3[bk[0]], po3[bk[1]]]; po_b = [po3_b[bk[0]], po3_b[bk[1]]]
                for half in range(2):
                    for fc in range(NFC):
                        mm(P, po[half][:], fT[:, fc, sb * 128:(sb + 1) * 128],
                           wd[:, fc, half * 512:(half + 1) * 512], fc == 0, fc == NFC - 1,
                           [fT_b[fc], wd_b[2 * half], wd_b[2 * half + 1]], [po_b[half]])
                    act(P, junk[:], po[half][:], AF.Square, [po_b[half]],
                        [s_b], accum_out=s_[:, half:half + 1])
                resid_tail(P, s_, s_b, po, po_b, gpost, gpost_b, tmp[i2], tmp_b[i2],
                           np_.hx[hb][:, sb, :], np_.hx_b[hb][sb],
                           h_out[ti * 128:(ti + 1) * 128, :], h_out_b[ti], 0.5)


def sb_proj_stage(cx, C, h_in, h_in_b, TL, wqkv_d, pre_gT_d, qT_d, kT_d, v_d, out_b):
    P = cx.P
    NT = TL // 512
    with ExitStack() as st:
        P.barrier()
        W = ChunkedW(cx, st, "wqkv", 8, 3 * D, 256)
        w = W.t
        pg = cx.sb([128, 8], F32, "pre_g", st); pg_b = Buf("pre_g")
        np_ = NormPipe(cx, C, st, h_in, h_in_b)
        np_.loads(0)
        P.dma("sp", pg[:], pre_gT_d, writes=[pg_b], sem=pg_b)
        W.load(wqkv_d, (pg, pg_b), lambda c0: (128 ** -0.5 if c0 < D else None))
        qkst = [cx.sb([128, 8, 512], BF16, "qkst", st) for _ in range(2)]
        qkst_b = [Buf("qkst") for _ in range(2)]
        vst = [cx.sb([128, 8, 4, 128], BF16, "vst", st) for _ in range(2)]
        vst_b = [Buf("vst") for _ in range(2)]
        pp = [cx.ps([128, 512], F32, "pp", st) for _ in range(4)]; pp_b = [Buf() for _ in range(4)]
        pi = 0
        qi = 0
        np_.act_part(0)
        for sb in range(4):
            np_.pe_part(0, sb)
        for T in range(NT):
            hb = T % 2
            xnT, xnT_b = np_.xnT, np_.xnT_b
            if T + 1 < NT:
                np_.loads(T + 1)
                np_.act_part(T + 1)
            for which, dst in ((0, qT_d), (1, kT_d)):
                qb = qi % 2; qi += 1
                for h in range(8):
                    p_ = pp[pi % 4]; p_b = pp_b[pi % 4]; pi += 1
                    c0 = which * D + h * 128
                    for k in range(8):
                        mm(P, p_[:], w[:, k, c0:c0 + 128], xnT[hb][:, k, :], k == 0, k == 7,
                           W.rd(c0, c0 + 128) + [xnT_b[hb]], [p_b])
                    if T + 1 < NT and which == 1 and h in (0, 2, 4, 6):
                        np_.pe_part(T + 1, h // 2)
                    if h % 2 == 0:
                        act(P, qkst[qb][:, h, :], p_[:], AF.Copy, [p_b], [qkst_b[qb]])
                    else:
                        P.op("dve", lambda e, qb=qb, h=h, p_=p_: e.tensor_copy(
                            out=qkst[qb][:, h, :], in_=p_[:]), [p_b], [qkst_b[qb]],
                            par=(h > 1))
                P.dma("sp", dst[:, :, T * 512:(T + 1) * 512].rearrange("h d t -> d h t"),
                      qkst[qb][:], reads=[qkst_b[qb]], writes=[out_b], sem=qkst_b[qb], par=True)
            vb = T % 2
            for sb in range(4):
                for half in range(2):
                    p_ = pp[pi % 4]; p_b = pp_b[pi % 4]; pi += 1
                    for k in range(8):
                        mm(P, p_[:], xnT[hb][:, k, sb * 128:(sb + 1) * 128],
                           w[:, k, 2 * D + half * 512:2 * D + (half + 1) * 512], k == 0, k == 7,
                           W.rd(2 * D + half * 512, 2 * D + (half + 1) * 512) + [xnT_b[hb]], [p_b])
                    o_ap = vst[vb][:, half * 4:(half + 1) * 4, sb, :]
                    i_ap = p_[:].rearrange("p (h d) -> p h d", h=4)
                    if half == 0:
                        act(P, o_ap, i_ap, AF.Copy, [p_b], [vst_b[vb]])
                    else:
                        P.op("dve", lambda e, o_ap=o_ap, i_ap=i_ap: e.tensor_copy(
                            out=o_ap, in_=i_ap), [p_b], [vst_b[vb]], par=True)
            P.dma("sp", v_d[:, :, T * 4:(T + 1) * 4, :].rearrange("h p j d -> p h j d"),
                  vst[vb][:], reads=[vst_b[vb]], writes=[out_b], sem=vst_b[vb], par=True)


def sb_attn_stage(cx, C, TL, qT_d, kTg_d, vg_d, in_b, oT_d, out_b, also_reads=None):
    P = cx.P
    NJ = TL // 128
    NG = NJ // 4
    with ExitStack() as st:
        P.barrier()
        KT = [cx.sb([128, 4, TL], BF16, "KT", st) for _ in range(2)]
        V = [cx.sb([128, 4, NJ, 128], BF16, "V", st) for _ in range(2)]
        QT = [cx.sb([128, TL], BF16, "QT", st) for _ in range(2)]
        kv_b = [Buf("kv") for _ in range(2)]
        OT = [cx.sb([128, TL], BF16, "OT", st) for _ in range(2)]
        OT_b = [Buf("OT") for _ in range(2)]
        e_t = [cx.sb([128, 1024], F32, "e", st) for _ in range(3)]; e_b = [Buf("e") for _ in range(3)]
        a_t = [cx.sb([128, 1024], F32, "a", st) for _ in range(2)]; a_b = [Buf("a") for _ in range(2)]
        sp_t = [cx.sb([128, 1024], BF16, "sp", st) for _ in range(2)]; sp_b = [Buf("sp") for _ in range(2)]
        at_t = [cx.sb([128, 1024], BF16, "att", st) for _ in range(2)]; at_b = [Buf("att") for _ in range(2)]
        sf_t = [cx.sb([128, 512], BF16, "sbf", st) for _ in range(2)]; sf_b = [Buf("sbf") for _ in range(2)]
        Z = cx.ps([128, 1024], F32, "Z", st); Z_b = Buf("Z")
        R = cx.ps([128, 1024], F32, "R", st); R_b = Buf("R")
        S = [cx.ps([128, 512], F32, "S", st) for _ in range(2)]; S_b = [Buf("S") for _ in range(2)]
        O = [cx.ps([128, 512], F32, "O", st) for _ in range(2)]; O_b = [Buf("O") for _ in range(2)]

        def load_head(h):
            i = h % 2
            P.dma("sp", KT[i][:], kTg_d[h].rearrange("r d t -> d r t"),
                  reads=_rd(_ib(in_b, h), also_reads), writes=[kv_b[i]], sem=kv_b[i])
            P.dma("sp", V[i][:], vg_d[h].rearrange("r p j d -> p r j d"),
                  reads=_rd(_ib(in_b, h), also_reads), writes=[kv_b[i]], sem=kv_b[i], par=True)
            P.dma("sp", QT[i][:], qT_d[h, :, :], reads=_rd(_ib(in_b, h), also_reads),
                  writes=[kv_b[i]], sem=kv_b[i], par=True)

        load_head(0)
        sidx = 0
        gidx = 0
        for h in range(8):
            hi = h % 2
            if h + 1 < 8:
                load_head(h + 1)
            kt, v, qt, kvb = KT[hi], V[hi], QT[hi], kv_b[hi]
            for G in range(NG):
                gi = gidx % 2; gidx += 1
                q0 = G * 512
                units = [(rp, jp) for jp in range(4 * G + 3, -1, -1) for rp in (3, 2, 1, 0)]
                steps = [units[i:i + 2] for i in range(0, len(units), 2)]

                def c0_of(u):
                    return max(0, u[1] - 4 * G) * 128

                def view(t, step):
                    c0 = c0_of(step[0])
                    if c0 == 0:
                        return t[:, 0:1024]
                    return t[:, 0:1024].rearrange("p (a t) -> p a t", a=2)[:, :, c0:512]

                def stage_z(step):
                    for i, (rp, jp) in enumerate(step):
                        diag = jp >= 4 * G
                        c0 = c0_of((rp, jp))
                        o = i * 512
                        lhs = kt[:, rp, jp * 128:(jp + 1) * 128]
                        if diag:
                            mm(P, Z[:, o + c0:o + c0 + 128], lhs, qt[:, q0 + c0:q0 + c0 + 128], True, False,
                               [kvb], [Z_b])
                            mm(P, Z[:, o + c0:o + c0 + 128], C.ident, C.msk[:, 0, rp, :], False, True,
                               [C.b], [Z_b])
                            if c0 + 128 < 512:
                                mm(P, Z[:, o + c0 + 128:o + 512], lhs, qt[:, q0 + c0 + 128:q0 + 512],
                                   True, True, [kvb], [Z_b])
                        else:
                            mm(P, Z[:, o:o + 512], lhs, qt[:, q0:q0 + 512], True, True, [kvb], [Z_b])

                def stage_e(step, si):
                    ei = si % 3
                    act(P, view(e_t[ei], step), view(Z, step), AF.Exp, [Z_b], [e_b[ei]])

                def stage_l(step, si):
                    act(P, view(sp_t[si % 2], step), view(e_t[si % 3], step), AF.Ln, [e_b[si % 3]],
                        [sp_b[si % 2]], bias=1.0)

                def stage_r(step, si, first):
                    zi = si % 2
                    spt, spb = sp_t[zi], sp_b[zi]
                    sft, sfb = sf_t[si % 2], sf_b[si % 2]
                    new0 = (step[0][1] >= 4 * G) and step[0][0] == 3
                    for i, u in enumerate(step):
                        c0 = c0_of(u)
                        o = i * 512
                        rng = [(c0, c0 + 128, False), (c0 + 128, 512, True)] if new0 else [(c0, 512, True)]
                        for (a, b, has_s) in rng:
                            if a >= b:
                                continue
                            lst = [(C.ntri, spt[:, o + a:o + b], [C.b, spb])]
                            if has_s:
                                lst.append((C.nones, sft[:, a:b], [C.b, sfb]))
                            if i == 1:
                                lst.append((C.nones, spt[:, a:b], [C.b, spb]))
                            for k, (l_, r_, rd) in enumerate(lst):
                                mm(P, R[:, o + a:o + b], l_, r_, k == 0, k == len(lst) - 1, rd, [R_b])

                def stage_s(step, si):
                    spt, spb = sp_t[si % 2], sp_b[si % 2]
                    for i, u in enumerate(step):
                        c0 = c0_of(u)
                        mma(P, S[gi][:, c0:], C.ident, spt[:, i * 512 + c0:(i + 1) * 512], [C.b, spb],
                            [S_b[gi]])
                    c0 = c0_of(step[0])
                    ni = (si + 1) % 2
                    tcopy(P, "dve", sf_t[ni][:, c0:], S[gi][:, c0:], [S_b[gi]], [sf_b[ni]])

                def stage_a(step, si):
                    zi = si % 2
                    act(P, view(a_t[zi], step), view(R, step), AF.Exp, [R_b], [a_b[zi]])
                    tt(P, "dve", view(at_t[zi], step), view(e_t[si % 3], step), view(a_t[zi], step),
                       ALU.mult, [e_b[si % 3], a_b[zi]], [at_b[zi]])

                def stage_o(step, si):
                    zi = si % 2
                    for i, (rp, jp) in enumerate(step):
                        c0 = c0_of((rp, jp))
                        mma(P, O[gi][:, c0:], v[:, rp, jp, :], at_t[zi][:, i * 512 + c0:(i + 1) * 512],
                            [kvb, at_b[zi]], [O_b[gi]])

                ns = len(steps)
                base = sidx
                P.op("dve", lambda e, t=S[gi]: e.memset(t[:], 0.0), [], [S_b[gi]])
                P.op("dve", lambda e, t=O[gi]: e.memset(t[:], 0.0), [], [O_b[gi]])
                for i in range(-2, ns + 1):
                    if 0 <= i + 2 < ns:
                        stage_z(steps[i + 2])
                    if 0 <= i + 1 < ns:
                        stage_l(steps[i + 1], base + i + 1)
                    if 0 <= i - 1 < ns:
                        stage_o(steps[i - 1], base + i - 1)
                    if 0 <= i < ns:
                        stage_a(steps[i], base + i)
                    if 0 <= i + 1 < ns:
                        stage_r(steps[i + 1], base + i + 1, i + 1 == 0)
                        stage_s(steps[i + 1], base + i + 1)
                    if 0 <= i + 2 < ns:
                        stage_e(steps[i + 2], base + i + 2)
                sidx += ns
                P.op("dve", lambda e, gi=gi, hi=hi, q0=q0: e.tensor_copy(
                    out=OT[hi][:, q0:q0 + 512], in_=O[gi][:]), [O_b[gi]], [OT_b[hi]],
                    par=(G > 0))
            P.dma("sp", oT_d[h, :, :], OT[hi][:], reads=[OT_b[hi]], writes=[out_b],
                  sem=OT_b[hi], par=True)


def outproj_stage(cx, C, TL, fT_d, f_b, h_in, h_in_b, h_out, h_out_b, wout_d, post_g_d):
    P = cx.P
    NT = TL // 512
    with ExitStack() as st:
        P.barrier()
        w = cx.sb([128, 8, D], BF16, "wout", st); w_b = Buf("wout")
        gpost = cx.sb([128, D], F32, "gpost", st); gpost_b = Buf("gpost")
        stg = make_stage(cx, 4, 1024, st)
        P.dma("sp", gpost[:], post_g_d, writes=[gpost_b], sem=gpost_b)
        load_weight(cx, w, w_b, wout_d, 8, D, stg=stg)
        hx = [cx.sb([128, 4, D], F32, "hx", st) for _ in range(2)]
        hx_b = [[Buf("hx") for _ in range(4)] for _ in range(2)]
        ft = [cx.sb([128, 8, 512], BF16, "ft", st) for _ in range(2)]
        ft_b = [Buf("ft") for _ in range(2)]
        junk = cx.sb([128, 512], BF16, "junk", st)
        small = [cx.sb([128, 8], F32, "small", st) for _ in range(4)]
        small_b = [Buf("small") for _ in range(4)]
        tmp = [cx.sb([128, D], F32, "tmp", st) for _ in range(2)]; tmp_b = [Buf("tmp") for _ in range(2)]
        po4 = [cx.ps([128, 512], F32, "po", st) for _ in range(4)]; po4_b = [Buf() for _ in range(4)]
        sidx = 0
        for T in range(NT):
            hb = T % 2
            for sb in range(4):
                ti = T * 4 + sb
                P.dma("sp", hx[hb][:, sb, :], h_in[ti * 128:(ti + 1) * 128, :],
                      reads=[h_in_b[ti]], writes=[hx_b[hb][sb]], sem=hx_b[hb][sb])
            P.dma("sp", ft[hb][:], fT_d[:, :, T * 512:(T + 1) * 512].rearrange("c f t -> f c t"),
                  reads=[f_b], writes=[ft_b[hb]], sem=ft_b[hb])
            for sb in range(4):
                ti = T * 4 + sb
                s_ = small[sidx % 4]; s_b = small_b[sidx % 4]; sidx += 1
                i2 = sb % 2
                po = po4[2 * i2:2 * i2 + 2]; po_b = po4_b[2 * i2:2 * i2 + 2]
                for half in range(2):
                    for c in range(8):
                        mm(P, po[half][:], ft[hb][:, c, sb * 128:(sb + 1) * 128],
                           w[:, c, half * 512:(half + 1) * 512], c == 0, c == 7,
                           [ft_b[hb], w_b], [po_b[half]])
                    act(P, junk[:], po[half][:], AF.Square, [po_b[half]], [s_b],
                        accum_out=s_[:, half:half + 1])
                resid_tail(P, s_, s_b, po, po_b, gpost, gpost_b, tmp[i2], tmp_b[i2],
                           hx[hb][:, sb, :], hx_b[hb][sb], h_out[ti * 128:(ti + 1) * 128, :],
                           h_out_b[ti], 1.0)


import os
HYB_DBG = int(os.environ.get('HYB_DBG', '0'))
C_DQ, C_DK, C_DV, C_GQ, C_GK, C_GV, C_GG, C_GA = 0, 512, 1024, 1536, 1792, 2048, 2560, 3072


def hyb_proj_stage(cx, C, h_in, h_in_b, TL, win_d, pre_gT_d, wa2_d, ba_d, gnorm_d,
                   dqT_d, dkT_d, dv_d, qeT_d, oi_d, G_d, U_d, dd_d, out_b):
    P = cx.P
    NT = TL // 512
    NJ = TL // 128
    with ExitStack() as st:
        P.barrier()
        W = ChunkedW(cx, st, "win", 8, HYB_IN, 256)
        w = W.t
        pg = cx.sb([128, 8], F32, "pre_g", st); pg_b = Buf("pre_g")
        np_ = NormPipe(cx, C, st, h_in, h_in_b)
        np_.loads(0)
        P.dma("sp", pg[:], pre_gT_d, writes=[pg_b], sem=pg_b)
        W.load(win_d, (pg, pg_b),
               lambda c0: (0.125 if (c0 < C_DK or C_GQ <= c0 < C_GK) else None),
               order=[0, 1, 2, 3, 4, 5, 12, 6, 7, 8, 9, 10, 11])
        load_consts_f32(cx, C, cx.cstf_d, st)
        wa2 = cx.sb([16, 256], F32, "wa2", st)
        ba = cx.sb([1, 256], F32, "ba", st)
        gn = cx.sb([128, 512], F32, "gnorm", st)
        misc_b = Buf("misc")
        P.dma("sp", wa2[:], wa2_d, writes=[misc_b], sem=misc_b)
        P.dma("sp", ba[:], ba_d, writes=[misc_b], sem=misc_b, par=True)
        P.dma("sp", gn[:], gnorm_d, writes=[misc_b], sem=misc_b, par=True)

        qkst = [cx.sb([128, 4, 512], BF16, "qkst", st) for _ in range(2)]
        qkst_b = [Buf("qkst") for _ in range(2)]
        vst = [cx.sb([128, 4, 4, 130], BF16, "vst", st) for _ in range(2)]
        vst_b = [Buf("vst") for _ in range(2)]
        qest = [cx.sb([128, 2, 512], BF16, "qest", st) for _ in range(2)]
        qest_b = [Buf("qest") for _ in range(2)]
        dall = cx.sb([128, NJ, 2], F32, "dall", st); dall_b = Buf("dall")
        def two(shape, dt, nm):
            return [cx.sb(shape, dt, nm, st) for _ in range(2)], [Buf(nm) for _ in range(2)]
        gaT, gaT_b = two([16, 128], F32, "gaT")
        e_a, e_a_b = two([128, 256], F32, "e_a")
        sp_a, sp_a_b = two([128, 256], F32, "sp_a")
        ek, ek_b = two([128, 256], F32, "ek")
        ketok, ketok_b = two([128, 256], BF16, "ketok")
        eqT, eqT_b = two([128, 2, 128], F32, "eqT")
        ekT, ekT_b = two([128, 2, 128], F32, "ekT")
        keT, keT_b = two([128, 2, 128], BF16, "keT")
        qbd, qbd_b = two([128, 2, 2, 128], BF16, "qbd")
        for i in range(2):
            P.op("pool", lambda e, t=qbd[i]: e.memset(t[:], 0.0), [], [qbd_b[i]])
        vtok, vtok_b = two([128, 512], BF16, "vtok")
        gsil, gsil_b = two([128, 512], F32, "gsil")
        atb, atb_b = two([128, 4, 128], BF16, "atb")
        oisb, oisb_b = two([128, 512], F32, "oisb")
        u2, u2_b = two([128, 2, 128], F32, "u2")
        pp = [cx.ps([128, 512], F32, "pp", st) for _ in range(6)]; pp_b = [Buf() for _ in range(6)]
        for i in range(2):
            P.op("dve", lambda e, t=vst[i]: e.memset(t[:, :, :, 128:129], 1.0), [], [vst_b[i]])
            P.op("dve", lambda e, t=vst[i]: e.memset(t[:, :, :, 129:130], 0.0), [], [vst_b[i]],
                 par=True)
        st_ = {"pi": 0, "sidx": 0}

        def bank():
            i = st_["pi"] % 6
            st_["pi"] += 1
            return pp[i], pp_b[i]

        np_.act_part(0)
        for sb in range(4):
            np_.pe_part(0, sb)
        for T in range(NT):
            hb = T % 2
            xT, xT_b = np_.xnT[hb], np_.xnT_b[hb]
            if T + 1 < NT:
                np_.loads(T + 1)
                np_.act_part(T + 1)
            for which, dst in ((0, dqT_d), (1, dkT_d)):
                qb = (2 * T + which) % 2
                for h in range(4):
                    p_, p_b = bank()
                    c0 = (C_DQ if which == 0 else C_DK) + h * 128
                    for k in range(8):
                        mm(P, p_[:], w[:, k, c0:c0 + 128], xT[:, k, :], k == 0, k == 7,
                           W.rd(c0, c0 + 128) + [xT_b], [p_b])
                    if h % 2 == 0:
                        act(P, qkst[qb][:, h, :], p_[:], AF.Copy, [p_b], [qkst_b[qb]])
                    else:
                        tcopy(P, "dve", qkst[qb][:, h, :], p_[:], [p_b], [qkst_b[qb]], par=(h > 1))
                P.dma("sp", dst[:, :, T * 512:(T + 1) * 512].rearrange("h d t -> d h t"),
                      qkst[qb][:], reads=[qkst_b[qb]], writes=[out_b], sem=qkst_b[qb], par=True)
            vb = T % 2
            for sb in range(4):
                p_, p_b = bank()
                for k in range(8):
                    mm(P, p_[:], xT[:, k, sb * 128:(sb + 1) * 128], w[:, k, C_DV:C_DV + 512],
                       k == 0, k == 7, W.rd(C_DV, C_DV + 512) + [xT_b], [p_b])
                o_ap = vst[vb][:, :, sb, 0:128]
                i_ap = p_[:].rearrange("p (h d) -> p h d", h=4)
                if sb % 2 == 0:
                    act(P, o_ap, i_ap, AF.Copy, [p_b], [vst_b[vb]])
                else:
                    tcopy(P, "dve", o_ap, i_ap, [p_b], [vst_b[vb]], par=True)
            P.dma("sp", dv_d[:, :, T * 4:(T + 1) * 4, :].rearrange("h p j d -> p h j d"),
                  vst[vb][:], reads=[vst_b[vb]], writes=[out_b], sem=vst_b[vb], par=True)
            qeb = T % 2
            for sb in range(4):
                j = T * 4 + sb
                bi = j % 2
                cols = slice(sb * 128, (sb + 1) * 128)
                if HYB_DBG and HYB_DBG <= 1:
                    continue
                if T + 1 < NT:
                    np_.pe_part(T + 1, sb)
                p1, p1_b = bank()
                for k in range(8):
                    mm(P, p1[0:16, 0:128], w[:, k, C_GA:C_GA + 16], xT[:, k, cols], k == 0, k == 7,
                       W.rd(C_GA, C_GA + 16) + [xT_b], [p1_b])
                tcopy(P, "dve", gaT[bi][:], p1[0:16, 0:128], [p1_b], [gaT_b[bi]])
                p2, p2_b = bank()
                mm(P, p2[:, 0:256], gaT[bi][:], wa2[:], True, False, [gaT_b[bi], misc_b], [p2_b])
                mm(P, p2[:, 0:256], C.onesf[0:1, :], ba[:], False, True, [C.fb, misc_b], [p2_b])
                act(P, e_a[bi][:], p2[:, 0:256], AF.Exp, [p2_b], [e_a_b[bi]], scale=-1.0)
                act(P, sp_a[bi][:], e_a[bi][:], AF.Ln, [e_a_b[bi]], [sp_a_b[bi]], bias=1.0)
                if HYB_DBG and HYB_DBG <= 2:
                    continue
                p3, p3_b = bank()
                mm(P, p3[:, 0:256], C.triu01f, sp_a[bi][:], True, True, [C.fb, sp_a_b[bi]], [p3_b])
                for hp in range(2):
                    mm(P, p3[:, 256 + hp * 128:384 + hp * 128], sp_a[bi][:, hp * 128:(hp + 1) * 128],
                       C.triu01f, True, True, [C.fb, sp_a_b[bi]], [p3_b])
                act(P, ek[bi][:], p3[:, 0:256], AF.Exp, [p3_b], [ek_b[bi]], scale=1.0 / 16)
                p3T = p3[:, 256:512].rearrange("p (a t) -> p a t", a=2)
                act(P, eqT[bi][:], p3T, AF.Exp, [p3_b], [eqT_b[bi]], scale=-1.0 / 16)
                act(P, ekT[bi][:], p3T, AF.Exp, [p3_b], [ekT_b[bi]], scale=1.0 / 16)
                if HYB_DBG and HYB_DBG <= 3:
                    continue
                p4, p4_b = bank()
                for k in range(8):
                    mm(P, p4[:, 0:256], xT[:, k, cols], w[:, k, C_GK:C_GK + 256], k == 0, k == 7,
                       W.rd(C_GK, C_GK + 256) + [xT_b], [p4_b])
                tt(P, "dve", ketok[bi][:], p4[:, 0:256], ek[bi][:], ALU.mult, [p4_b, ek_b[bi]],
                   [ketok_b[bi]])
                p5, p5_b = bank()
                for hp in range(2):
                    for which, cb in ((0, C_GQ), (1, C_GK)):
                        o_ = p5[:, (2 * which + hp) * 128:(2 * which + hp + 1) * 128]
                        for k in range(8):
                            mm(P, o_, w[:, k, cb + hp * 128:cb + (hp + 1) * 128], xT[:, k, cols],
                               k == 0, k == 7, W.rd(cb, cb + 256) + [xT_b], [p5_b])
                p5v = p5[:].rearrange("p (a t) -> p a t", a=4)
                tt(P, "dve", qest[qeb][:, :, cols], p5v[:, 0:2, :], eqT[bi][:], ALU.mult,
                   [p5_b, eqT_b[bi]], [qest_b[qeb]], par=(sb > 0))
                tt(P, "dve", keT[bi][:], p5v[:, 2:4, :], ekT[bi][:], ALU.mult,
                   [p5_b, ekT_b[bi]], [keT_b[bi]])
                for hh in range(2):
                    rows = slice(hh * 64, (hh + 1) * 64)
                    tt(P, "dve", qbd[bi][rows, :, hh, :], p5v[rows, 0:2, :], eqT[bi][rows, :, :],
                       ALU.mult, [p5_b, eqT_b[bi]], [qbd_b[bi]], par=(hh > 0))
                tcopy(P, "dve", dall[:, j, :], eqT[bi][:, :, 127], [eqT_b[bi]], [dall_b], par=True)
                if HYB_DBG and HYB_DBG <= 4:
                    continue
                p6, p6_b = bank()
                for k in range(8):
                    mm(P, p6[:], xT[:, k, cols], w[:, k, C_GV:C_GV + 512], k == 0, k == 7,
                       W.rd(C_GV, C_GV + 512) + [xT_b], [p6_b])
                act(P, vtok[bi][:], p6[:], AF.Copy, [p6_b], [vtok_b[bi]])
                p7, p7_b = bank()
                for k in range(8):
                    mm(P, p7[:], xT[:, k, cols], w[:, k, C_GG:C_GG + 512], k == 0, k == 7,
                       W.rd(C_GG, C_GG + 512) + [xT_b], [p7_b])
                act(P, gsil[bi][:], p7[:], AF.Silu, [p7_b], [gsil_b[bi]])
                tt(P, "pool", gsil[bi][:], gsil[bi][:], gn[:], ALU.mult, [gsil_b[bi], misc_b],
                   [gsil_b[bi]])
                P.dma("sp", G_d[j * 128:(j + 1) * 128, :], gsil[bi][:], reads=[gsil_b[bi]],
                      writes=[out_b], sem=gsil_b[bi], par=True)
                if HYB_DBG and HYB_DBG <= 5:
                    continue
                p8, p8_b = bank()
                for hp in range(2):
                    mm(P, p8[:, hp * 256:(hp + 1) * 256], keT[bi][:, hp, :],
                       qbd[bi][:, hp, :, :].rearrange("p a t -> p (a t)"), True, True,
                       [keT_b[bi], qbd_b[bi]], [p8_b])
                if HYB_DBG == 7:
                    continue
                tt(P, "dve", atb[bi][:], p8[:].rearrange("p (h t) -> p h t", h=4), C.triu01x4,
                   ALU.mult, [p8_b, C.b], [atb_b[bi]])
                if HYB_DBG == 8:
                    continue
                p9, p9_b = bank()
                for h in range(4):
                    mm(P, p9[:, h * 128:(h + 1) * 128], atb[bi][:, h, :],
                       vtok[bi][:, h * 128:(h + 1) * 128], True, True,
                       [atb_b[bi], vtok_b[bi]], [p9_b])
                if HYB_DBG == 9:
                    continue
                act(P, oisb[bi][:], p9[:], AF.Copy, [p9_b], [oisb_b[bi]])
                P.dma("sp", oi_d[j * 128:(j + 1) * 128, :], oisb[bi][:], reads=[oisb_b[bi]],
                      writes=[out_b], sem=oisb_b[bi], par=True)
                if HYB_DBG and HYB_DBG <= 6:
                    continue
                p10, p10_b = bank()
                for h in range(4):
                    hp = h // 2
                    mm(P, p10[:, h * 128:(h + 1) * 128], ketok[bi][:, hp * 128:(hp + 1) * 128],
                       vtok[bi][:, h * 128:(h + 1) * 128], True, True,
                       [ketok_b[bi], vtok_b[bi]], [p10_b])
                for h in range(4):
                    hp, hh = divmod(h, 2)
                    rows = slice(hh * 64, (hh + 1) * 64)
                    ts(P, "dve", u2[bi][rows, hp, :], p10[rows, h * 128:(h + 1) * 128],
                       eqT[bi][rows, hp, 127:128], None, ALU.mult, None,
                       [p10_b, eqT_b[bi]], [u2_b[bi]], par=(h > 0))
                P.dma("sp", U_d[j, :, :], u2[bi][:].rearrange("p a d -> p (a d)"),
                      reads=[u2_b[bi]], writes=[out_b], sem=u2_b[bi], par=True)
            P.dma("sp", qeT_d[:, :, T * 512:(T + 1) * 512].rearrange("a d t -> d a t"),
                  qest[qeb][:], reads=[qest_b[qeb]], writes=[out_b], sem=qest_b[qeb], par=True)
        P.dma("sp", dd_d, dall[:], reads=[dall_b], writes=[out_b], sem=dall_b, par=True)


def diff_attn_stage(cx, C, TL, li_d, dqT_d, dkTg_d, dvg_d, in_b, lam_d, subg_d, fT_d, out_b,
                    also_reads=None):
    P = cx.P
    NJ = TL // 128
    NG = NJ // 4
    with ExitStack() as st:
        P.barrier()
        KT = [cx.sb([128, 4, TL], BF16, "KT", st) for _ in range(2)]
        V = [cx.sb([128, 4, NJ, 130], BF16, "V", st) for _ in range(2)]
        QT = [[cx.sb([128, TL], BF16, "QT", st) for _ in range(2)] for _ in range(2)]
        kv_b = [Buf("kv") for _ in range(2)]
        for i in range(2):
            P.op("pool", lambda e, t=QT[i][0]: e.memset(t[64:128, :], 0.0), [], [kv_b[i]])
            P.op("pool", lambda e, t=QT[i][1]: e.memset(t[0:64, :], 0.0), [], [kv_b[i]], par=True)
        YT = [cx.sb([128, TL], BF16, "YT", st) for _ in range(2)]
        YT_b = [Buf("YT") for _ in range(2)]
        pt = [[cx.sb([128, 512], BF16, "pt", st) for _ in range(2)] for _ in range(2)]
        pt_b = [[Buf("pt") for _ in range(2)] for _ in range(2)]
        lp = cx.sb([128, 256], F32, "lp", st)
        lt = cx.sb([128, 128], F32, "lt", st)
        lsm = cx.sb([128, 8], F32, "lsm", st)
        gsub = cx.sb([128, 128], F32, "gsub", st)
        lam_b = Buf("lam")
        a_t = [cx.sb([128, 128], F32, "a", st) for _ in range(2)]; a_b = [Buf("a") for _ in range(2)]
        y_t = [cx.sb([128, 128], BF16, "y", st) for _ in range(2)]; y_b = [Buf("y") for _ in range(2)]
        junk = cx.sb([128, 128], BF16, "junk", st)
        sm = [cx.sb([128, 8], F32, "sm", st) for _ in range(2)]; sm_b = [Buf("sm") for _ in range(2)]
        Z = [[cx.ps([128, 512], F32, "Z", st) for _ in range(2)] for _ in range(2)]
        Z_b = [[Buf("Z") for _ in range(2)] for _ in range(2)]
        ACC = [cx.ps([128, 3, 130], F32, "ACC", st) for _ in range(3)]
        ACC_b = [Buf("ACC") for _ in range(3)]
        tpy = cx.ps([128, 128], BF16, "tpy", st); tpy_b = Buf("tpy")

        P.dma("sp", lp[:], lam_d, writes=[lam_b], sem=lam_b)
        P.dma("sp", gsub[:], subg_d, writes=[lam_b], sem=lam_b, par=True)
        tt(P, "dve", lt[:, 0:64], lp[:, 0:64], lp[:, 64:128], ALU.mult, [lam_b], [lam_b])
        tt(P, "dve", lt[:, 64:128], lp[:, 128:192], lp[:, 192:256], ALU.mult, [lam_b], [lam_b])
        P.op("dve", lambda e: e.reduce_sum(lsm[:, 0:1], lt[:, 0:64], AX.X), [lam_b], [lam_b])
        P.op("dve", lambda e: e.reduce_sum(lsm[:, 1:2], lt[:, 64:128], AX.X), [lam_b], [lam_b])
        act(P, lsm[:, 2:4], lsm[:, 0:2], AF.Exp, [lam_b], [lam_b])
        tt(P, "dve", lsm[:, 4:5], lsm[:, 3:4], lsm[:, 2:3], ALU.subtract, [lam_b], [lam_b])
        li = cx.sb([128, 2], F32, "li", st)
        P.dma("sp", li[:], li_d, writes=[lam_b], sem=lam_b, par=True)
        tt(P, "dve", lsm[:, 5:6], lsm[:, 4:5], li[:, 0:1], ALU.add, [lam_b], [lam_b])
        ts(P, "dve", gsub[:], gsub[:], li[:, 1:2], None, ALU.mult, None, [lam_b], [lam_b])
        nlam = lsm[:, 5:6]

        def load_head(h):
            i = h % 2
            P.dma("sp", KT[i][:], dkTg_d[h].rearrange("r d t -> d r t"),
                  reads=_rd(_ib(in_b, h), also_reads), writes=[kv_b[i]], sem=kv_b[i])
            for half in range(2):
                P.dma("sp", V[i][half * 64:(half + 1) * 64], dvg_d[h, half].rearrange(
                    "r p j d -> p r j d"), reads=_rd(_ib(in_b, h), also_reads), writes=[kv_b[i]],
                    sem=kv_b[i], par=True)
            P.dma("sp", QT[i][0][0:64, :], dqT_d[h, 0:64, :], reads=_rd(_ib(in_b, h), also_reads), writes=[kv_b[i]],
                  sem=kv_b[i], par=True)
            P.dma("sp", QT[i][1][64:128, :], dqT_d[h, 64:128, :], reads=_rd(_ib(in_b, h), also_reads), writes=[kv_b[i]],
                  sem=kv_b[i], par=True)

        def acc(m, qb):
            i = m * 4 + qb
            return ACC[i // 3][:, i % 3, :], ACC_b[i // 3]

        load_head(0)
        uidx = 0
        fin = 0
        for h in range(4):
            hi = h % 2
            if h + 1 < 4:
                load_head(h + 1)
            kt, v, qt, kvb = KT[hi], V[hi], QT[hi], kv_b[hi]
            for G in range(NG):
                q0 = G * 512
                units = [(rp, jp) for jp in range(4 * G + 3, -1, -1) for rp in (3, 2, 1, 0)]
                for i in range(3):
                    P.op("dve", lambda e, t=ACC[i]: e.memset(t[:], 0.0), [], [ACC_b[i]])

                def stage_z(u, ui):
                    zi = ui % 2
                    rp, jp = u
                    diag = jp >= 4 * G
                    c0 = max(0, jp - 4 * G) * 128
                    ranges = [(c0, c0 + 128, True), (c0 + 128, 512, False)] if diag else [(0, 512, False)]
                    for m in range(2):
                        for (a, b, isd) in ranges:
                            if a >= b:
                                continue
                            mm(P, Z[m][zi][:, a:b], kt[:, rp, jp * 128:(jp + 1) * 128],
                               qt[m][:, q0 + a:q0 + b], True, not isd, [kvb], [Z_b[m][zi]])
                            if isd:
                                mm(P, Z[m][zi][:, a:b], C.ident, C.msk[:, 1, rp, :], False, True,
                                   [C.b], [Z_b[m][zi]])
                        act(P, pt[m][zi][:, c0:], Z[m][zi][:, c0:], AF.Exp, [Z_b[m][zi]],
                            [pt_b[m][zi]])

                def stage_o(u, ui):
                    zi = ui % 2
                    rp, jp = u
                    c0 = max(0, jp - 4 * G) * 128
                    for m in range(2):
                        for qb in range(c0 // 128, 4):
                            a_ap, a_bf = acc(m, qb)
                            mma(P, a_ap, pt[m][zi][:, qb * 128:(qb + 1) * 128], v[:, rp, jp, :],
                                [pt_b[m][zi], kvb], [a_bf])

                nu = len(units)
                stage_z(units[0], uidx)
                for i in range(nu):
                    if i + 1 < nu:
                        stage_z(units[i + 1], uidx + i + 1)
                    stage_o(units[i], uidx + i)
                uidx += nu
                for qb in range(4):
                    fi = fin % 2; fin += 1
                    s_ = sm[fi]; s_b = sm_b[fi]
                    a0, a0_b = acc(0, qb)
                    a1, a1_b = acc(1, qb)
                    P.op("dve", lambda e, s_=s_, a0=a0: e.reciprocal(s_[:, 0:1], a0[:, 128:129]),
                         [a0_b], [s_b])
                    P.op("dve", lambda e, s_=s_, a1=a1: e.reciprocal(s_[:, 1:2], a1[:, 128:129]),
                         [a1_b], [s_b])
                    tt(P, "dve", s_[:, 2:3], s_[:, 1:2], nlam, ALU.mult, [s_b, lam_b], [s_b])
                    ts(P, "dve", a_t[fi][:], a0[:, 0:128], s_[:, 0:1], None, ALU.mult, None,
                       [a0_b, s_b], [a_b[fi]])
                    stt(P, a_t[fi][:], a1[:, 0:128], s_[:, 2:3], a_t[fi][:], ALU.mult, ALU.add,
                        [a1_b, s_b, a_b[fi]], [a_b[fi]])
                    act(P, junk[:], a_t[fi][:], AF.Square, [a_b[fi]], [s_b], accum_out=s_[:, 3:4])
                    rstd_from_ss(P, s_[:, 4:5], s_[:, 3:4], 128, [s_b], [s_b])
                    stt(P, y_t[fi][:], a_t[fi][:], s_[:, 4:5], gsub[:], ALU.mult, ALU.mult,
                        [a_b[fi], s_b, lam_b], [y_b[fi]])
                    P.op("pe", lambda e, fi=fi: e.transpose(tpy[:], y_t[fi][:], C.ident),
                         [y_b[fi], C.b], [tpy_b])
                    c = q0 + qb * 128
                    tcopy(P, "dve", YT[hi][:, c:c + 128], tpy[:], [tpy_b], [YT_b[hi]],
                          par=not (G == 0 and qb == 0))
            P.dma("sp", fT_d[h, :, :], YT[hi][:], reads=[YT_b[hi]], writes=[out_b],
                  sem=YT_b[hi], par=True)


def gla_scan_stage(cx, C, TL, Ug_d, ddg_d, in_b, qeT_d, oi_d, G_d, onehot_d, fT_d, out_b,
                   also_reads=None):
    P = cx.P
    NJ = TL // 128
    with ExitStack() as st:
        P.barrier()
        dd = cx.sb([128, 4, NJ, 2], F32, "dd", st)
        oh = cx.sb([128, 4], F32, "oh", st)
        qe = cx.sb([128, 2, TL], BF16, "qe", st)
        c_b = Buf("gc")
        P.dma("sp", dd[:], ddg_d.rearrange("r p j a -> p r j a"), reads=_rd(_ib(in_b, "dd"), also_reads), writes=[c_b], sem=c_b)
        P.dma("sp", oh[:], onehot_d, writes=[c_b], sem=c_b, par=True)
        P.dma("sp", qe[:], qeT_d.rearrange("a d t -> d a t"), reads=_rd(_ib(in_b, "dd"), also_reads), writes=[c_b],
              sem=c_b, par=True)
        S2 = cx.sb([128, 2, 128], F32, "S2", st); S2_b = Buf("S2")
        YT = cx.sb([128, 4, TL], BF16, "YT", st); YT_b = Buf("YT")
        def two(shape, dt, nm):
            return [cx.sb(shape, dt, nm, st) for _ in range(2)], [Buf(nm) for _ in range(2)]
        Uj, Uj_b = two([128, 4, 256], F32, "Uj")
        so, so_b = two([128, 2, 128], F32, "sown")
        sob, sob_b = two([128, 2, 2, 128], BF16, "sownb")
        oi, oi_b = two([128, 512], F32, "oi")
        Gt, Gt_b = two([128, 512], F32, "Gt")
        ot, ot_b = two([128, 512], F32, "ot")
        yt, yt_b = two([128, 512], BF16, "yt")
        sm, sm_b = two([128, 8], F32, "sm")
        junk = cx.sb([128, 128], BF16, "junk", st)
        po = [cx.ps([128, 512], F32, "po", st) for _ in range(2)]; po_b = [Buf() for _ in range(2)]
        tpy = [cx.ps([128, 4, 128], BF16, "tpy", st) for _ in range(2)]; tpy_b = [Buf() for _ in range(2)]
        P.op("dve", lambda e: e.memset(S2[:], 0.0), [], [S2_b])
        for i in range(2):
            P.op("pool", lambda e, t=sob[i]: e.memset(t[:], 0.0), [], [sob_b[i]])
        for j in range(NJ):
            bi = j % 2
            JC = Ug_d.shape[2]
            P.dma("sp", Uj[bi][:], Ug_d[j // JC, :, j % JC, :, :].rearrange("r p c -> p r c"),
                  reads=_rd(_ib(in_b, ("u", j // JC)), also_reads), writes=[Uj_b[bi]], sem=Uj_b[bi])
            P.dma("sp", oi[bi][:], oi_d[j * 128:(j + 1) * 128, :], reads=_rd(_ib(in_b, "dd"), also_reads), writes=[oi_b[bi]],
                  sem=oi_b[bi])
            P.dma("sp", Gt[bi][:], G_d[j * 128:(j + 1) * 128, :], reads=_rd(_ib(in_b, "dd"), also_reads), writes=[Gt_b[bi]],
                  sem=Gt_b[bi])
            for rp in range(4):
                if rp == 0:
                    ts(P, "dve", so[bi][:], S2[:], oh[:, 0:1], None, ALU.mult, None,
                       [S2_b, c_b], [so_b[bi]])
                else:
                    stt(P, so[bi][:], S2[:], oh[:, rp:rp + 1], so[bi][:], ALU.mult, ALU.add,
                        [S2_b, c_b, so_b[bi]], [so_b[bi]])
                for hp in range(2):
                    stt(P, S2[:, hp, :], S2[:, hp, :], dd[:, rp, j, hp:hp + 1],
                        Uj[bi][:, rp, hp * 128:(hp + 1) * 128], ALU.mult, ALU.add,
                        [S2_b, c_b, Uj_b[bi]], [S2_b])
            for hh in range(2):
                rows = slice(hh * 64, (hh + 1) * 64)
                tcopy(P, "pool", sob[bi][rows, :, hh, :], so[bi][rows, :, :], [so_b[bi]],
                      [sob_b[bi]], par=(hh > 0))
            for hp in range(2):
                mm(P, po[bi][:, hp * 256:(hp + 1) * 256], qe[:, hp, j * 128:(j + 1) * 128],
                   sob[bi][:, hp, :, :].rearrange("p a d -> p (a d)"), True, True,
                   [c_b, sob_b[bi]], [po_b[bi]])
            tt(P, "dve", ot[bi][:], po[bi][:], oi[bi][:], ALU.add, [po_b[bi], oi_b[bi]], [ot_b[bi]])
            for h in range(4):
                act(P, junk[:], ot[bi][:, h * 128:(h + 1) * 128], AF.Square, [ot_b[bi]],
                    [sm_b[bi]], accum_out=sm[bi][:, h:h + 1])
            rstd_from_ss(P, sm[bi][:, 4:8], sm[bi][:, 0:4], 128, [sm_b[bi]], [sm_b[bi]])
            for h in range(4):
                cs = slice(h * 128, (h + 1) * 128)
                stt(P, yt[bi][:, cs], ot[bi][:, cs], sm[bi][:, 4 + h:5 + h], Gt[bi][:, cs],
                    ALU.mult, ALU.mult, [ot_b[bi], sm_b[bi], Gt_b[bi]], [yt_b[bi]], par=(h > 0))
            for h in range(4):
                P.op("pe", lambda e, bi=bi, h=h: e.transpose(
                    tpy[bi][:, h, :], yt[bi][:, h * 128:(h + 1) * 128], C.ident),
                    [yt_b[bi], C.b], [tpy_b[bi]])
            tcopy(P, "dve", YT[:, :, j * 128:(j + 1) * 128], tpy[bi][:], [tpy_b[bi]], [YT_b],
                  par=(j > 0))
        P.dma("sp", fT_d[4:8, :, :].rearrange("c f t -> f c t"), YT[:], reads=[YT_b],
              writes=[out_b], sem=YT_b, par=True)


def _ffn_inputs(cx, tag):
    return dict(
        wg=cx.dram(tag + "_wg", [128, 8, DFF], F32, "ExternalInput"),
        wu=cx.dram(tag + "_wu", [128, 8, DFF], F32, "ExternalInput"),
        wd=cx.dram(tag + "_wd", [128, NFC, D], F32, "ExternalInput"),
        pre=cx.dram(tag + "_pre", [128, 8], F32, "ExternalInput"),
        post=cx.dram(tag + "_post", [128, D], F32, "ExternalInput"))


def build_A(kind, TL):
    NJ = TL // 128
    cx = Ctx()
    h_in = cx.dram("h_in", [TL, D], F32, "ExternalInput")
    h1 = cx.dram("h1", [TL, D], F32, "ExternalOutput")
    cst = cx.dram("cst", [128, 8, 128], BF16, "ExternalInput")
    msk = cx.dram("msk", [128, 2, 4, 128], BF16, "ExternalInput")
    f = _ffn_inputs(cx, "f0")
    mpre = cx.dram("mpre", [128, 8], F32, "ExternalInput")
    C = load_consts(cx, cst, msk)
    hb0 = [Buf() for _ in range(NJ)]
    hb1 = [Buf() for _ in range(NJ)]
    ob = Buf()
    if kind == "sb":
        wqkv = cx.dram("wqkv", [128, 8, 3 * D], F32, "ExternalInput")
        qT = cx.dram("qT", [8, 128, TL], BF16, "ExternalOutput")
        kT = cx.dram("kT", [8, 128, TL], BF16, "ExternalOutput")
        v = cx.dram("v", [8, 128, NJ, 128], BF16, "ExternalOutput")
        ffn_stage(cx, C, h_in, hb0, h1, hb1, TL, f["wg"], f["wu"], f["wd"], f["pre"], f["post"])
        sb_proj_stage(cx, C, h1, hb1, TL, wqkv, mpre, qT, kT, v, ob)
    else:
        win = cx.dram("win", [128, 8, HYB_IN], F32, "ExternalInput")
        cx.cstf_d = cx.dram("cstf", [128, 2, 128], F32, "ExternalInput")
        wa2 = cx.dram("wa2", [16, 256], F32, "ExternalInput")
        ba = cx.dram("ba", [1, 256], F32, "ExternalInput")
        gnorm = cx.dram("gnorm", [128, 512], F32, "ExternalInput")
        dqT = cx.dram("dqT", [4, 128, TL], BF16, "ExternalOutput")
        dkT = cx.dram("dkT", [4, 128, TL], BF16, "ExternalOutput")
        dv = cx.dram("dv", [4, 128, NJ, 130], BF16, "ExternalOutput")
        qeT = cx.dram("qeT", [2, 128, TL], BF16, "ExternalOutput")
        oi = cx.dram("oi", [TL, 512], F32, "ExternalOutput")
        G = cx.dram("G", [TL, 512], F32, "ExternalOutput")
        U = cx.dram("U", [NJ, 128, 256], F32, "ExternalOutput")
        dd = cx.dram("dd", [128, NJ, 2], F32, "ExternalOutput")
        ffn_stage(cx, C, h_in, hb0, h1, hb1, TL, f["wg"], f["wu"], f["wd"], f["pre"], f["post"])
        hyb_proj_stage(cx, C, h1, hb1, TL, win, mpre, wa2, ba, gnorm, dqT, dkT, dv, qeT, oi, G, U,
                       dd, ob)
    cx.P.finalize(cx.stack)
    return cx


def build_B(kind, TL):
    NJ = TL // 128
    cx = Ctx()
    h1 = cx.dram("h1", [TL, D], F32, "ExternalInput")
    h2 = cx.dram("h2", [TL, D], F32, "Internal")
    h3 = cx.dram("h3", [TL, D], F32, "ExternalOutput")
    fT = cx.dram("fT", [8, 128, TL], BF16, "Internal")
    cst = cx.dram("cst", [128, 8, 128], BF16, "ExternalInput")
    msk = cx.dram("msk", [128, 2, 4, 128], BF16, "ExternalInput")
    wout = cx.dram("wout", [128, 8, D], F32, "ExternalInput")
    mpost = cx.dram("mpost", [128, D], F32, "ExternalInput")
    f = _ffn_inputs(cx, "f1")
    C = load_consts(cx, cst, msk)
    hb1 = [Buf() for _ in range(NJ)]
    hb2 = [Buf() for _ in range(NJ)]
    hb3 = [Buf() for _ in range(NJ)]
    fb = Buf()
    ib = Buf()
    if kind == "sb":
        qT = cx.dram("qT", [8, 128, TL], BF16, "ExternalInput")
        kTg = cx.dram("kTg", [8, 4, 128, TL], BF16, "ExternalInput")
        vg = cx.dram("vg", [8, 4, 128, NJ, 128], BF16, "ExternalInput")
        sb_attn_stage(cx, C, TL, qT, kTg, vg, ib, fT, fb)
    else:
        dqT = cx.dram("dqT", [4, 128, TL], BF16, "ExternalInput")
        dkTg = cx.dram("dkTg", [4, 4, 128, TL], BF16, "ExternalInput")
        dvg = cx.dram("dvg", [4, 2, 4, 64, NJ, 130], BF16, "ExternalInput")
        lam = cx.dram("lam", [128, 256], F32, "ExternalInput")
        subg = cx.dram("subg", [128, 128], F32, "ExternalInput")
        li = cx.dram("li", [128, 2], F32, "ExternalInput")
        JC = min(8, NJ)
        Ug = cx.dram("Ug", [NJ // JC, 4, JC, 128, 256], F32, "ExternalInput")
        ddg = cx.dram("ddg", [4, 128, NJ, 2], F32, "ExternalInput")
        qeT = cx.dram("qeT", [2, 128, TL], BF16, "ExternalInput")
        oi = cx.dram("oi", [TL, 512], F32, "ExternalInput")
        G = cx.dram("G", [TL, 512], F32, "ExternalInput")
        onehot = cx.dram("onehot", [128, 4], F32, "ExternalInput")
        diff_attn_stage(cx, C, TL, li, dqT, dkTg, dvg, ib, lam, subg, fT, fb)
        gla_scan_stage(cx, C, TL, Ug, ddg, ib, qeT, oi, G, onehot, fT, fb)
    outproj_stage(cx, C, TL, fT, fb, h1, hb1, h2, hb2, wout, mpost)
    ffn_stage(cx, C, h2, hb2, h3, hb3, TL, f["wg"], f["wu"], f["wd"], f["pre"], f["post"])
    cx.P.finalize(cx.stack)
    return cx


GROUPS = [[0, 1, 2, 3], [4, 5, 6, 7]]


def build_fused(TL, depth):
    NJ = TL // 128
    cx = Ctx()
    P = cx.P
    ein = lambda n, s, d=F32: cx.dram(n, s, d, "ExternalInput")
    itn = lambda n, s, d=F32: cx.dram(n, s, d, "Internal")
    x_in = ein("x", [TL, D])
    out = cx.dram("out", [TL, D], F32, "ExternalOutput")
    cst = ein("cst", [128, 8, 128], BF16)
    msk = ein("msk", [128, 2, 4, 128], BF16)
    cx.cstf_d = ein("cstf", [128, 2, 128])
    onehot = ein("onehot", [128, 4])
    C = load_consts(cx, cst, msk)
    hs = [itn("hs%d" % i, [TL, D]) for i in range(3)]
    hs_b = [[Buf() for _ in range(NJ)] for _ in range(3)]
    fT = itn("fT", [8, 128, TL], BF16); fT_b = Buf("fT")
    qT = itn("qT", [8, 128, TL], BF16)
    kT = itn("kT", [8, 128, TL], BF16)
    v = itn("v", [8, 128, NJ, 128], BF16)
    kTg = itn("kTg", [8, 4, 128, TL], BF16)
    vg = itn("vg", [8, 4, 128, NJ, 128], BF16)
    dqT = itn("dqT", [4, 128, TL], BF16)
    dkT = itn("dkT", [4, 128, TL], BF16)
    dv = itn("dv", [4, 128, NJ, 130], BF16)
    qeT = itn("qeT", [2, 128, TL], BF16)
    oi = itn("oi", [TL, 512])
    G = itn("G", [TL, 512])
    U = itn("U", [NJ, 128, 256])
    dd = itn("dd", [128, NJ, 2])
    dkTg = itn("dkTg", [4, 4, 128, TL], BF16)
    dvg = itn("dvg", [4, 2, 4, 64, NJ, 130], BF16)
    JC = min(8, NJ)
    Ug = itn("Ug", [NJ // JC, 4, JC, 128, 256])
    ddg = itn("ddg", [4, 128, NJ, 2])
    loc_b = Buf("local")
    gat_sb = {h: Buf("gath_sb%d" % h) for h in range(8)}
    gat_hyb = {h: Buf("gath_hyb%d" % h) for h in range(4)}
    gat_hyb["dd"] = Buf("gath_dd")
    for jc in range(NJ // min(8, NJ)):
        gat_hyb[("u", jc)] = Buf("gath_u%d" % jc)

    def flat2(ap, pat, **kw):
        return ap.rearrange(pat, **kw)

    def gather(src, dst, spat, dpat, gb):
        P.cc_allgather(dst.rearrange(dpat), src.rearrange(spat), GROUPS, reads=[loc_b],
                       writes=[gb], sem=gb)

    cur, cur_b = x_in, [Buf() for _ in range(NJ)]
    for L in range(depth):
        kind = "hyb" if L % 2 == 0 else "sb"
        e = L // 2
        f0 = _ffn_inputs(cx, "L%d_f0" % L)
        f1 = _ffn_inputs(cx, "L%d_f1" % L)
        mpre = ein("L%d_mpre" % L, [128, 8])
        mpost = ein("L%d_mpost" % L, [128, D])
        wout = ein("L%d_wout" % L, [128, 8, D])
        h1, h1_b = hs[0], hs_b[0]
        h2, h2_b = hs[1], hs_b[1]
        if L == depth - 1:
            h3, h3_b = out, [Buf() for _ in range(NJ)]
        else:
            h3, h3_b = hs[2], hs_b[2]
        ffn_stage(cx, C, cur, cur_b, h1, h1_b, TL, f0["wg"], f0["wu"], f0["wd"], f0["pre"], f0["post"])
        if kind == "sb":
            wqkv = ein("L%d_wqkv" % L, [128, 8, 3 * D])
            sb_proj_stage(cx, C, h1, h1_b, TL, wqkv, mpre, qT, kT, v, loc_b)
            for hh in range(8):
                gather(kT[hh], kTg[hh], "d t -> d t", "r d t -> (r d) t", gat_sb[hh])
                gather(v[hh], vg[hh], "p j d -> p (j d)", "r p j d -> (r p) (j d)", gat_sb[hh])
            sb_attn_stage(cx, C, TL, qT, kTg, vg, gat_sb, fT, fT_b, also_reads=loc_b)
        else:
            win = ein("L%d_win" % L, [128, 8, HYB_IN])
            wa2 = ein("L%d_wa2" % L, [16, 256])
            ba = ein("L%d_ba" % L, [1, 256])
            gnorm = ein("L%d_gnorm" % L, [128, 512])
            lam = ein("L%d_lam" % L, [128, 256])
            subg = ein("L%d_subg" % L, [128, 128])
            li = ein("L%d_li" % L, [128, 2])
            hyb_proj_stage(cx, C, h1, h1_b, TL, win, mpre, wa2, ba, gnorm, dqT, dkT, dv, qeT, oi,
                           G, U, dd, loc_b)
            for hh in range(4):
                gather(dkT[hh], dkTg[hh], "d t -> d t", "r d t -> (r d) t", gat_hyb[hh])
                for half in range(2):
                    gather(dv[hh, half * 64:(half + 1) * 64], dvg[hh, half], "p j d -> p (j d)",
                           "r p j d -> (r p) (j d)", gat_hyb[hh])
            gather(dd, ddg, "p j a -> p (j a)", "r p j a -> (r p) (j a)", gat_hyb["dd"])
            for jc in range(NJ // JC):
                gather(U[jc * JC:(jc + 1) * JC], Ug[jc], "j p c -> (j p) c", "r j p c -> (r j p) c",
                       gat_hyb[("u", jc)])
            diff_attn_stage(cx, C, TL, li, dqT, dkTg, dvg, gat_hyb, lam, subg, fT, fT_b,
                            also_reads=loc_b)
            gla_scan_stage(cx, C, TL, Ug, ddg, gat_hyb, qeT, oi, G, onehot, fT, fT_b,
                           also_reads=loc_b)
        outproj_stage(cx, C, TL, fT, fT_b, h1, h1_b, h2, h2_b, wout, mpost)
        ffn_stage(cx, C, h2, h2_b, h3, h3_b, TL, f1["wg"], f1["wu"], f1["wd"], f1["pre"], f1["post"])
        cur, cur_b = h3, h3_b
    cx.P.finalize(cx.stack)
    return cx


def _wl(W):
    kc = W.shape[0] // 128
    return np.ascontiguousarray(W.reshape(kc, 128, W.shape[1]).transpose(1, 0, 2))


def _colT(g):
    return np.ascontiguousarray(g.reshape(8, 128).T)


def _rep(v, n=128):
    return np.ascontiguousarray(np.broadcast_to(v, (n,) + v.shape))


def _ffn_maps(tag, wg, wu, wd, pre, post):
    return {tag + "_wg": _wl(wg), tag + "_wu": _wl(wu), tag + "_wd": _wl(wd),
            tag + "_pre": _colT(pre), tag + "_post": _rep(post)}


def _shard_tokens(x, NJ):
    out = []
    for c in range(8):
        b, r = divmod(c, 4)
        out.append(np.ascontiguousarray(
            x[b].reshape(NJ, 4, 128, x.shape[-1])[:, r].reshape(NJ * 128, x.shape[-1])))
    return out


def _unshard_tokens(lst, S, NJ):
    out = np.zeros((2, S, lst[0].shape[-1]), lst[0].dtype)
    for c in range(8):
        b, r = divmod(c, 4)
        out[b].reshape(NJ, 4, 128, lst[0].shape[-1])[:, r] = lst[c].reshape(NJ, 128, -1)
    return out


_PROGS = {}


def _prog(which, kind, TL):
    k = (which, kind, TL)
    if k not in _PROGS:
        _PROGS[k] = build_A(kind, TL) if which == "A" else build_B(kind, TL)
    return _PROGS[k]


def _run(cx, in_maps):
    return run_bass_kernel_spmd(cx.nc, in_maps, core_ids=list(range(8))).results


def kernel_unfused(x, ffn_pre_g, ffn_post_g, ffn_w_gate, ffn_w_up, ffn_w_down, mix_pre_g, mix_post_g,
           hyb_w_in, hyb_w_out, diff_lambda, diff_subln_g, gla_w_a2, gla_b_a, gla_norm_g,
           sb_w_qkv, sb_w_out):
    f32 = lambda a: np.asarray(a, dtype=np.float32)
    x = f32(x)
    S = x.shape[1]
    NJ = S // 512
    TL = NJ * 128
    depth = mix_pre_g.shape[0]
    cst = host_consts()
    cstf = host_consts_f32()
    msks = [host_masks(c % 4) for c in range(8)]
    h = _shard_tokens(x, NJ)
    for L in range(depth):
        kind = "hyb" if L % 2 == 0 else "sb"
        e = L // 2
        common = dict(cst=cst)
        a_in = dict(common)
        a_in.update(_ffn_maps("f0", f32(ffn_w_gate[L, 0]), f32(ffn_w_up[L, 0]),
                              f32(ffn_w_down[L, 0]), f32(ffn_pre_g[L, 0]), f32(ffn_post_g[L, 0])))
        a_in["mpre"] = _colT(f32(mix_pre_g[L]))
        if kind == "sb":
            a_in["wqkv"] = _wl(f32(sb_w_qkv[e]))
        else:
            a_in["win"] = _wl(f32(hyb_w_in[e]))
            a_in["cstf"] = cstf
            a_in["wa2"] = np.ascontiguousarray(f32(gla_w_a2[e]))
            a_in["ba"] = np.ascontiguousarray(f32(gla_b_a[e]).reshape(1, 256))
            a_in["gnorm"] = _rep(np.tile(f32(gla_norm_g[e]), 4))
        rA = _run(_prog("A", kind, TL),
                  [dict(a_in, h_in=h[c], msk=msks[c]) for c in range(8)])
        b_in = dict(common)
        b_in.update(_ffn_maps("f1", f32(ffn_w_gate[L, 1]), f32(ffn_w_up[L, 1]),
                              f32(ffn_w_down[L, 1]), f32(ffn_pre_g[L, 1]), f32(ffn_post_g[L, 1])))
        b_in["wout"] = _wl(f32(sb_w_out[e] if kind == "sb" else hyb_w_out[e]))
        b_in["mpost"] = _rep(f32(mix_post_g[L]))
        maps = []
        for c in range(8):
            b, r = divmod(c, 4)
            grp = [rA[4 * b + rp] for rp in range(4)]
            m = dict(b_in, h1=rA[c]["h1"], msk=msks[c])
            if kind == "sb":
                m["qT"] = rA[c]["qT"]
                m["kTg"] = np.stack([g["kT"] for g in grp], axis=1)
                m["vg"] = np.stack([g["v"] for g in grp], axis=1)
            else:
                lambda_init = 0.8 - 0.6 * math.exp(-0.3 * L)
                oh = np.zeros((128, 4), np.float32)
                oh[:, r] = 1.0
                li = np.zeros((128, 2), np.float32)
                li[:, 0] = -lambda_init
                li[:, 1] = 1.0 - lambda_init
                JC = min(8, NJ)
                dvs = np.stack([g["dv"].reshape(4, 2, 64, NJ, 130) for g in grp], axis=2)
                Us = np.stack([g["U"].reshape(NJ // JC, JC, 128, 256) for g in grp], axis=1)
                m.update(dqT=rA[c]["dqT"], dkTg=np.stack([g["dkT"] for g in grp], axis=1),
                         dvg=dvs,
                         lam=_rep(f32(diff_lambda[e]).reshape(256)),
                         subg=_rep(f32(diff_subln_g[e])), li=li,
                         Ug=Us, ddg=np.stack([g["dd"] for g in grp]),
                         qeT=rA[c]["qeT"], oi=rA[c]["oi"], G=rA[c]["G"], onehot=oh)
            maps.append(m)
        rB = _run(_prog("B", kind, TL), maps)
        h = [rB[c]["h3"] for c in range(8)]
    return _unshard_tokens(h, S, NJ).astype(np.float32)


def kernel(x, ffn_pre_g, ffn_post_g, ffn_w_gate, ffn_w_up, ffn_w_down, mix_pre_g, mix_post_g,
           hyb_w_in, hyb_w_out, diff_lambda, diff_subln_g, gla_w_a2, gla_b_a, gla_norm_g,
           sb_w_qkv, sb_w_out):
    f32 = lambda a: np.asarray(a, dtype=np.float32)
    x = f32(x)
    S = x.shape[1]
    NJ = S // 512
    TL = NJ * 128
    depth = mix_pre_g.shape[0]
    key = ("fused", TL, depth)
    if key not in _PROGS:
        _PROGS[key] = build_fused(TL, depth)
    cx = _PROGS[key]
    shared = dict(cst=host_consts(), cstf=host_consts_f32())
    for L in range(depth):
        e = L // 2
        t = "L%d_" % L
        for i in range(2):
            shared.update(_ffn_maps(t + "f%d" % i, f32(ffn_w_gate[L, i]), f32(ffn_w_up[L, i]),
                                    f32(ffn_w_down[L, i]), f32(ffn_pre_g[L, i]),
                                    f32(ffn_post_g[L, i])))
        shared[t + "mpre"] = _colT(f32(mix_pre_g[L]))
        shared[t + "mpost"] = _rep(f32(mix_post_g[L]))
        if L % 2 == 1:
            shared[t + "wqkv"] = _wl(f32(sb_w_qkv[e]))
            shared[t + "wout"] = _wl(f32(sb_w_out[e]))
        else:
            lambda_init = 0.8 - 0.6 * math.exp(-0.3 * L)
            li = np.zeros((128, 2), np.float32)
            li[:, 0] = -lambda_init
            li[:, 1] = 1.0 - lambda_init
            shared[t + "win"] = _wl(f32(hyb_w_in[e]))
            shared[t + "wout"] = _wl(f32(hyb_w_out[e]))
            shared[t + "wa2"] = np.ascontiguousarray(f32(gla_w_a2[e]))
            shared[t + "ba"] = np.ascontiguousarray(f32(gla_b_a[e]).reshape(1, 256))
            shared[t + "gnorm"] = _rep(np.tile(f32(gla_norm_g[e]), 4))
            shared[t + "lam"] = _rep(f32(diff_lambda[e]).reshape(256))
            shared[t + "subg"] = _rep(f32(diff_subln_g[e]))
            shared[t + "li"] = li
    xs = _shard_tokens(x, NJ)
    maps = []
    for c in range(8):
        oh = np.zeros((128, 4), np.float32)
        oh[:, c % 4] = 1.0
        maps.append(dict(shared, x=xs[c], msk=host_masks(c % 4), onehot=oh))
    res = _run(cx, maps)
    return _unshard_tokens([res[c]["out"] for c in range(8)], S, NJ).astype(np.float32)
```
